# Optimizing a Trainium2 kernel written in Bass

```python
import jax, jax.numpy as jnp
from jax import lax
import numpy as np

D_MODEL = 2048
BATCH = 4
SEQ = 2048
DEPTH = 4
DEC_BATCH = 8
DEC_SEQ = 16
PAST_LEN = 4096

CHUNK = 64
HEAD_DIM = 64
D_MIX = D_MODEL
SWA_WIDTH = D_MIX // 4
SWA_HEADS = SWA_WIDTH // HEAD_DIM
SWA_KV_HEADS = SWA_HEADS // 4
SWA_GROUP = SWA_HEADS // SWA_KV_HEADS
SWA_KV_WIDTH = SWA_KV_HEADS * HEAD_DIM
WINDOW = 128
SWA_PREV_CHUNKS = WINDOW // CHUNK
D_LRU = D_MIX // 2
LRU_BLOCK_W = 64
LRU_BLOCKS = D_LRU // LRU_BLOCK_W
LRU_C = 8.0
CONV_WIDTH = 4
BAND_WIDTH = D_MIX - SWA_WIDTH - D_LRU
BAND_HEADS = BAND_WIDTH // HEAD_DIM
BAND_PREV_CHUNKS = 8
BAND_REACH = BAND_PREV_CHUNKS * CHUNK
REL_CLIP = 128
D_FF = 4 * D_MODEL
N_MOD = 6
RMS_EPS = 1e-6
NEG_INF = -1e30
PROJ_SIZES = (SWA_WIDTH, SWA_KV_WIDTH, SWA_KV_WIDTH, D_LRU, D_LRU, BAND_WIDTH, BAND_WIDTH, BAND_WIDTH)
D_IN = sum(PROJ_SIZES)
PROJ_SPLITS = tuple(int(s) for s in np.cumsum(PROJ_SIZES)[:-1])

kernel_name = 'hybrid_stream_encoder_step'


def rmsnorm(x, g):
    xf = x.astype(jnp.float32)
    y = xf * lax.rsqrt(jnp.mean(xf * xf, axis=-1, keepdims=True) + RMS_EPS)
    return (y * g.astype(jnp.float32)).astype(x.dtype)


def ada_mod(c, w_mod, b_mod):
    m = (jax.nn.silu(c) @ w_mod + b_mod).reshape(c.shape[0], N_MOD, 1, D_MODEL)
    return [m[:, j] for j in range(N_MOD)]


def mixer_input(x, shift, scale, g_pre, w_in):
    return (rmsnorm(x, g_pre) * (1 + scale) + shift) @ w_in


def split_proj(z):
    b, t = z.shape[:2]
    qa, ka, va, xb, gb, qc, kc, vc = jnp.split(z, PROJ_SPLITS, axis=-1)
    return (qa.reshape(b, t, SWA_KV_HEADS, SWA_GROUP, HEAD_DIM),
            ka.reshape(b, t, SWA_KV_HEADS, HEAD_DIM),
            va.reshape(b, t, SWA_KV_HEADS, HEAD_DIM),
            xb, gb,
            qc.reshape(b, t, BAND_HEADS, 1, HEAD_DIM),
            kc.reshape(b, t, BAND_HEADS, HEAD_DIM),
            vc.reshape(b, t, BAND_HEADS, HEAD_DIM))


def attend(q, k, v, bias, sink):
    s = jnp.einsum('bnqhgd,bnkhd->bnhgqk', q, k).astype(jnp.float32) * (HEAD_DIM ** -0.5)
    if bias is not None:
        s = s + bias.astype(jnp.float32)
    if sink is not None:
        sink_col = jnp.broadcast_to(sink.astype(jnp.float32)[:, :, None, None], s.shape[:-1] + (1,))
        p = jax.nn.softmax(jnp.concatenate([s, sink_col], axis=-1), axis=-1)[..., :-1]
    else:
        p = jax.nn.softmax(s, axis=-1)
    return jnp.einsum('bnhgqk,bnkhd->bnqhgd', p.astype(v.dtype), v)


def band_gather(t, n_prev):
    T = t.shape[1]
    pad = n_prev * CHUNK
    tp = jnp.pad(t, ((0, 0), (pad, 0)) + ((0, 0),) * (t.ndim - 2))
    idx = (jnp.arange(T // CHUNK) * CHUNK)[:, None] + jnp.arange(pad + CHUNK)[None, :]
    return tp[:, idx], idx >= pad


def chunk_queries(q):
    b, t = q.shape[:2]
    return q.reshape((b, t // CHUNK, CHUNK) + q.shape[2:])


def rel_bias(table, n_q, n_k, offset):
    d = jnp.arange(n_q)[:, None] + offset - jnp.arange(n_k)[None, :]
    return table[:, jnp.clip(d, -REL_CLIP, REL_CLIP) + REL_CLIP]


def swa_prompt(q, k, v, sink):
    b, t = q.shape[:2]
    kb, valid = band_gather(k, SWA_PREV_CHUNKS)
    vb, _ = band_gather(v, SWA_PREV_CHUNKS)
    mask = jnp.where(valid, 0.0, NEG_INF)[None, :, None, None, None, :]
    return attend(chunk_queries(q), kb, vb, mask, sink).reshape(b, t, SWA_WIDTH)


def swa_sample(q, k, v, ck, cv, sink):
    b, t = q.shape[:2]
    kk = jnp.concatenate([ck.astype(k.dtype), k], axis=1)[:, None]
    vv = jnp.concatenate([cv.astype(v.dtype), v], axis=1)[:, None]
    return attend(q[:, None], kk, vv, None, sink).reshape(b, t, SWA_WIDTH)


def band_prompt(q, k, v, table):
    b, t = q.shape[:2]
    kb, valid = band_gather(k, BAND_PREV_CHUNKS)
    vb, _ = band_gather(v, BAND_PREV_CHUNKS)
    mask = jnp.where(valid, 0.0, NEG_INF)[None, :, None, None, None, :]
    rb = rel_bias(table, CHUNK, BAND_REACH + CHUNK, BAND_REACH).astype(jnp.float32)[None, None, :, None]
    return attend(chunk_queries(q), kb, vb, rb + mask, None).reshape(b, t, BAND_WIDTH)


def band_sample(q, k, v, ck, cv, table):
    b, t = q.shape[:2]
    n_past = ck.shape[1]
    kk = jnp.concatenate([ck.astype(k.dtype), k], axis=1)[:, None]
    vv = jnp.concatenate([cv.astype(v.dtype), v], axis=1)[:, None]
    rb = rel_bias(table, t, n_past + t, n_past)[None, None, :, None]
    return attend(q[:, None], kk, vv, rb, None).reshape(b, t, BAND_WIDTH)


def lru_combine(left, right):
    a1, u1 = left
    a2, u2 = right
    return a1 * a2, a2 * u1 + u2


def rg_lru(xb, gb, conv_prefix, h0, conv_w, conv_b, w_r, b_r, w_i, b_i, lam):
    b, t, d = xb.shape
    x_ext = jnp.concatenate([conv_prefix.astype(xb.dtype), xb], axis=1)
    xc = conv_b
    for j in range(CONV_WIDTH):
        xc = xc + conv_w[j] * x_ext[:, j:j + t]
    xg = xc.reshape(b, t, LRU_BLOCKS, LRU_BLOCK_W)
    r = jax.nn.sigmoid((jnp.einsum('btnc,ncd->btnd', xg, w_r) + b_r).astype(jnp.float32)).reshape(b, t, d)
    i = jax.nn.sigmoid((jnp.einsum('btnc,ncd->btnd', xg, w_i) + b_i).astype(jnp.float32)).reshape(b, t, d)
    log_a = -LRU_C * r * jax.nn.softplus(-lam.astype(jnp.float32))
    a = jnp.exp(log_a)
    u = jnp.sqrt(-jnp.expm1(2.0 * log_a)) * (i * xc.astype(jnp.float32))
    u = u.at[:, 0].add(a[:, 0] * h0.astype(jnp.float32))
    _, h = lax.associative_scan(lru_combine, (a, u), axis=1)
    y = h.astype(xb.dtype) * jax.nn.gelu(gb)
    return y, h[:, -1].astype(xb.dtype), x_ext[:, -(CONV_WIDTH - 1):]


def finish_layer(x, mixed, gate_a, shift_m, scale_m, gate_m, w_out, g_post_mix, g_pre_mlp, w_up, w_down, g_post_mlp):
    x = x + gate_a * rmsnorm(mixed @ w_out, g_post_mix)
    h = rmsnorm(x, g_pre_mlp) * (1 + scale_m) + shift_m
    f = jnp.square(jax.nn.relu(h @ w_up)) @ w_down
    return x + gate_m * rmsnorm(f, g_post_mlp)


def setup_inputs(seed: int = 0) -> dict:
    key = jax.random.key(seed)
    ks = jax.random.split(key, 32)
    f32 = jnp.float32

    def nrm(k, shape, s):
        return s * jax.random.normal(k, shape, f32)

    swa_len = min(WINDOW, PAST_LEN)
    band_len = min(BAND_REACH, PAST_LEN)
    a_pow = jax.random.uniform(ks[21], (DEPTH, D_LRU), f32, 0.9, 0.999)
    a_base = a_pow ** (1.0 / LRU_C)
    lam = jnp.log(a_base) - jnp.log1p(-a_base)
    return {
        'x_prompt': nrm(ks[0], (BATCH, SEQ, D_MODEL), 1.0),
        'x_sample': nrm(ks[1], (DEC_BATCH, DEC_SEQ, D_MODEL), 1.0),
        'c_prompt': nrm(ks[2], (BATCH, D_MODEL), 1.0),
        'c_sample': nrm(ks[3], (DEC_BATCH, D_MODEL), 1.0),
        'cache_swa_k': nrm(ks[4], (DEPTH, DEC_BATCH, swa_len, SWA_KV_HEADS, HEAD_DIM), 1.0),
        'cache_swa_v': nrm(ks[5], (DEPTH, DEC_BATCH, swa_len, SWA_KV_HEADS, HEAD_DIM), 1.0),
        'cache_band_k': nrm(ks[6], (DEPTH, DEC_BATCH, band_len, BAND_HEADS, HEAD_DIM), 1.0),
        'cache_band_v': nrm(ks[7], (DEPTH, DEC_BATCH, band_len, BAND_HEADS, HEAD_DIM), 1.0),
        'state_lru_h': nrm(ks[8], (DEPTH, DEC_BATCH, D_LRU), 0.5),
        'state_lru_conv': nrm(ks[9], (DEPTH, DEC_BATCH, CONV_WIDTH - 1, D_LRU), 1.0),
        'w_mod': nrm(ks[10], (DEPTH, D_MODEL, N_MOD * D_MODEL), 0.5 * D_MODEL ** -0.5),
        'b_mod': nrm(ks[11], (DEPTH, N_MOD * D_MODEL), 0.01),
        'g_pre_mix': 1.0 + nrm(ks[12], (DEPTH, D_MODEL), 0.05),
        'w_in': nrm(ks[13], (DEPTH, D_MODEL, D_IN), D_MODEL ** -0.5),
        'swa_sink': nrm(ks[14], (DEPTH, SWA_HEADS), 0.5),
        'lru_conv_w': nrm(ks[15], (DEPTH, CONV_WIDTH, D_LRU), CONV_WIDTH ** -0.5),
        'lru_conv_b': nrm(ks[16], (DEPTH, D_LRU), 0.01),
        'lru_w_r': nrm(ks[17], (DEPTH, LRU_BLOCKS, LRU_BLOCK_W, LRU_BLOCK_W), LRU_BLOCK_W ** -0.5),
        'lru_b_r': nrm(ks[18], (DEPTH, LRU_BLOCKS, LRU_BLOCK_W), 0.01),
        'lru_w_i': nrm(ks[19], (DEPTH, LRU_BLOCKS, LRU_BLOCK_W, LRU_BLOCK_W), LRU_BLOCK_W ** -0.5),
        'lru_b_i': nrm(ks[20], (DEPTH, LRU_BLOCKS, LRU_BLOCK_W), 0.01),
        'lru_lambda': lam,
        'band_rel_bias': nrm(ks[22], (DEPTH, BAND_HEADS, 2 * REL_CLIP + 1), 0.1),
        'w_out': nrm(ks[23], (DEPTH, D_MIX, D_MODEL), D_MIX ** -0.5),
        'g_post_mix': 1.0 + nrm(ks[24], (DEPTH, D_MODEL), 0.05),
        'g_pre_mlp': 1.0 + nrm(ks[25], (DEPTH, D_MODEL), 0.05),
        'w_up': nrm(ks[26], (DEPTH, D_MODEL, D_FF), D_MODEL ** -0.5),
        'w_down': nrm(ks[27], (DEPTH, D_FF, D_MODEL), D_FF ** -0.5),
        'g_post_mlp': 1.0 + nrm(ks[28], (DEPTH, D_MODEL), 0.05),
    }


def reference(x_prompt, x_sample, c_prompt, c_sample, cache_swa_k, cache_swa_v, cache_band_k, cache_band_v,
              state_lru_h, state_lru_conv, w_mod, b_mod, g_pre_mix, w_in, swa_sink, lru_conv_w, lru_conv_b,
              lru_w_r, lru_b_r, lru_w_i, lru_b_i, lru_lambda, band_rel_bias, w_out, g_post_mix, g_pre_mlp,
              w_up, w_down, g_post_mlp):
    xp, xs = x_prompt, x_sample
    bp, tp = xp.shape[:2]
    swa_keep = min(WINDOW, tp)
    band_keep = min(BAND_REACH, tp)
    swa_kp, swa_vp, band_kp, band_vp, lru_hp, lru_cp = [], [], [], [], [], []
    swa_ks, swa_vs, band_ks, band_vs, lru_hs, lru_cs = [], [], [], [], [], []
    for l in range(DEPTH):
        sink = swa_sink[l].reshape(SWA_KV_HEADS, SWA_GROUP)
        lru_w = (lru_conv_w[l], lru_conv_b[l], lru_w_r[l], lru_b_r[l], lru_w_i[l], lru_b_i[l], lru_lambda[l])
        post_w = (w_out[l], g_post_mix[l], g_pre_mlp[l], w_up[l], w_down[l], g_post_mlp[l])

        sa, sc, ga, sm, scm, gm = ada_mod(c_prompt, w_mod[l], b_mod[l])
        qa, ka, va, xb, gb, qc, kc, vc = split_proj(mixer_input(xp, sa, sc, g_pre_mix[l], w_in[l]))
        oa = swa_prompt(qa, ka, va, sink)
        ob, h_last, conv_last = rg_lru(xb, gb, jnp.zeros((bp, CONV_WIDTH - 1, D_LRU), xb.dtype),
                                       jnp.zeros((bp, D_LRU), xb.dtype), *lru_w)
        oc = band_prompt(qc, kc, vc, band_rel_bias[l])
        xp = finish_layer(xp, jnp.concatenate([oa, ob, oc], axis=-1), ga, sm, scm, gm, *post_w)
        swa_kp.append(ka[:, tp - swa_keep:])
        swa_vp.append(va[:, tp - swa_keep:])
        band_kp.append(kc[:, tp - band_keep:])
        band_vp.append(vc[:, tp - band_keep:])
        lru_hp.append(h_last)
        lru_cp.append(conv_last)

        sa, sc, ga, sm, scm, gm = ada_mod(c_sample, w_mod[l], b_mod[l])
        qa, ka, va, xb, gb, qc, kc, vc = split_proj(mixer_input(xs, sa, sc, g_pre_mix[l], w_in[l]))
        oa = swa_sample(qa, ka, va, cache_swa_k[l], cache_swa_v[l], sink)
        ob, h_last, conv_last = rg_lru(xb, gb, state_lru_conv[l], state_lru_h[l], *lru_w)
        oc = band_sample(qc, kc, vc, cache_band_k[l], cache_band_v[l], band_rel_bias[l])
        xs = finish_layer(xs, jnp.concatenate([oa, ob, oc], axis=-1), ga, sm, scm, gm, *post_w)
        swa_ks.append(ka)
        swa_vs.append(va)
        band_ks.append(kc)
        band_vs.append(vc)
        lru_hs.append(h_last)
        lru_cs.append(conv_last)

    return (xp, xs,
            jnp.stack(swa_kp), jnp.stack(swa_vp), jnp.stack(band_kp), jnp.stack(band_vp),
            jnp.stack(lru_hp), jnp.stack(lru_cp),
            jnp.stack(swa_ks), jnp.stack(swa_vs), jnp.stack(band_ks), jnp.stack(band_vs),
            jnp.stack(lru_hs), jnp.stack(lru_cs))
```

```python
import contextlib
import numpy as np
import concourse.bass as bass
import concourse.mybir as mybir
from concourse.bass_utils import run_bass_kernel_spmd

F32 = mybir.dt.float32
BF16 = mybir.dt.bfloat16
AF = mybir.ActivationFunctionType
ALU = mybir.AluOpType
AX = mybir.AxisListType

NLAYER = 4
D = 2048
NCH = 16
T = 1024
TS = 16
NT = T + TS
WCOLS = 22 * 256
G = 256


class Res:
    __slots__ = ("name", "w", "r", "dsem", "dcount")

    def __init__(self, name):
        self.name = name
        self.w = {}
        self.r = {}
        self.dsem = None
        self.dcount = 0


class Eng:
    def __init__(self, name, sem):
        self.name = name
        self.sem = sem
        self.count = 0
        self.ops = []
        self.waited = {}


class K:
    ENGS = ["pe", "act", "dve", "pool", "sp"]

    def __init__(self, nc, stack):
        self.nc = nc
        self.stack = stack
        self.eng = {}
        for n in self.ENGS:
            self.eng[n] = Eng(n, stack.enter_context(nc.semaphore("s_" + n)))
        self.nsem = len(self.ENGS)
        self.semid = {}

    def sb(self, name, shape, dtype):
        return self.stack.enter_context(self.nc.sbuf_tensor(name, shape, dtype))

    def ps(self, name, shape, dtype=F32):
        return self.stack.enter_context(self.nc.psum_tensor(name, shape, dtype))

    def newsem(self, name):
        self.nsem += 1
        return self.stack.enter_context(self.nc.semaphore(name))

    def _collect(self, E, reads, writes, selfwait):
        waits = {}
        for R in reads:
            for sem, val in R.w.items():
                if waits.get(sem, 0) < val:
                    waits[sem] = val
        for R in writes:
            for sem, val in R.w.items():
                if waits.get(sem, 0) < val:
                    waits[sem] = val
            for sem, val in R.r.items():
                if waits.get(sem, 0) < val:
                    waits[sem] = val
        wl = []
        for sem, val in waits.items():
            if (sem is E.sem) and not selfwait:
                continue
            if E.waited.get(sem, 0) >= val:
                continue
            E.waited[sem] = val
            wl.append((sem, val))
        return wl

    @staticmethod
    def _mark(ev, reads, writes):
        s, v = ev
        for R in reads:
            if R.r.get(s, 0) < v:
                R.r[s] = v
        for R in writes:
            if R.w.get(s, 0) < v:
                R.w[s] = v

    def op(self, eng, fn, reads=(), writes=(), inc=True, selfwait=None):
        E = self.eng[eng]
        if selfwait is None:
            selfwait = eng != "pe"
        wl = self._collect(E, reads, writes, selfwait)
        if inc:
            E.count += 1
            ev = (E.sem, E.count)
        else:
            ev = (E.sem, E.count + 1)
        self._mark(ev, reads, writes)
        E.ops.append((wl, fn, (E.sem, 1) if inc else None))

    def dma(self, eng, fn, reads=(), writes=()):
        E = self.eng[eng]
        wl = self._collect(E, reads, writes, True)
        R0 = writes[0] if len(writes) else reads[0]
        if R0.dsem is None:
            R0.dsem = self.newsem("d_" + R0.name)
        R0.dcount += 16
        ev = (R0.dsem, R0.dcount)
        self._mark(ev, reads, writes)
        E.ops.append((wl, fn, (R0.dsem, 16)))

    def custom(self, eng, fn, sem, reads=(), writes=()):
        E = self.eng[eng]
        wl = self._collect(E, reads, writes, True)
        cnt = self.semid.get(id(sem), 0) + 1
        self.semid[id(sem)] = cnt
        self._mark((sem, cnt), reads, writes)
        E.ops.append((wl, fn, (sem, 1)))

    def wait_all(self, eng, ress):
        E = self.eng[eng]
        wl = self._collect(E, [], ress, True)
        E.ops.append((wl, None, None))

    def emit(self):
        nc = self.nc
        with nc.Block() as block:
            def replay(E, engine):
                for wl, fn, inc in E.ops:
                    for sem, val in wl:
                        engine.wait_ge(sem, val)
                    if fn is None:
                        continue
                    ins = fn(engine)
                    if inc is not None:
                        ins.then_inc(inc[0], inc[1])

            @block.tensor
            def _(e):
                replay(self.eng["pe"], e)

            @block.scalar
            def _(e):
                replay(self.eng["act"], e)

            @block.vector
            def _(e):
                replay(self.eng["dve"], e)

            @block.gpsimd
            def _(e):
                replay(self.eng["pool"], e)

            @block.sync
            def _(e):
                replay(self.eng["sp"], e)


class Arena:
    def __init__(self, k, name, nbytes):
        self.nbytes = nbytes
        self.t = k.sb(name, [128, nbytes // 4], F32)
        self.gr = [Res(f"{name}_g{i}") for i in range((nbytes + G - 1) // G)]


class Buf:
    def __init__(self, arena, off, dtype, shape):
        self.arena = arena
        self.off = off
        self.dtype = dtype
        self.shape = tuple(shape)
        self.esz = 2 if dtype == BF16 else 4
        n = int(np.prod(shape))
        self.nbytes = n * self.esz
        assert off % 4 == 0 and self.nbytes % 4 == 0, (off, self.nbytes)
        assert off + self.nbytes <= arena.nbytes, (off, self.nbytes, arena.nbytes)
        v = arena.t[:, off // 4:(off + self.nbytes) // 4]
        if dtype != F32:
            v = v.bitcast(dtype)
        if len(shape) == 2:
            v = v.rearrange("p (a b) -> p a b", a=shape[0])
        elif len(shape) == 3:
            v = v.rearrange("p (a b c) -> p a b c", a=shape[0], b=shape[1])
        self.ap = v

    def rs(self, lo=None, hi=None):
        if lo is None:
            lo, hi = 0, self.nbytes // self.esz
        b0 = self.off + lo * self.esz
        b1 = self.off + hi * self.esz
        return self.arena.gr[b0 // G:(b1 - 1) // G + 1]

    def rows(self, i, j=None):
        per = int(np.prod(self.shape[1:]))
        j = i + 1 if j is None else j
        return self.rs(i * per, j * per)


def build_program(nl, ncores):
    nc = bass.Bass("TRN2", target_bir_lowering=False)
    NSH = 12288 // ncores
    MCH = NSH // 128
    MBLK = NSH // 256

    def din(name, shape):
        return nc.dram_tensor(name, list(shape), F32, kind="ExternalInput").ap()

    def dout(name, shape):
        return nc.dram_tensor(name, list(shape), F32, kind="ExternalOutput").ap()

    xT_d = din("xT", [128, NCH * NT])
    cT_d = din("cT", [128, NCH * 12])
    wmod_d = din("wmod", [nl, D, NSH])
    bmod_d = din("bmod", [128, nl * 96])
    sel_d = din("sel", [128, 24])
    gv_d = din("gv", [128, nl * 4 * NCH])
    win_d = din("win", [nl, D, WCOLS])
    wout_d = din("wout", [nl, D, D])
    wup_d = din("wup", [nl, D, 4 * D])
    wdn_d = din("wdn", [nl, 4 * D, D])
    lrup_d = din("lrup", [128, nl * 8 * 8])
    wri_d = din("wri", [nl * 16 * 128, 128])
    biasp_d = din("biasp", [nl * 4 * 128, 576])
    biass_d = din("biass", [nl * 4 * 32, 528])
    sink_d = din("sink", [128, nl * 4])
    sinks_d = din("sinks", [32, nl * 4])
    flags_d = din("flags", [128, 2])
    ident_d = din("ident", [128, 128])
    cswak_d = din("cswak", [nl * 2 * 128, 128])
    cswav_d = din("cswav", [nl * 128, 256])
    cbk_d = din("cbk", [nl * 4 * 128, 512])
    cbv_d = din("cbv", [nl * 512, 512])
    sth_d = din("sth", [128, nl * 8])
    stc_d = din("stc", [128, nl * 8 * 3])

    yT_o = dout("yT", [128, NCH * NT])
    kc_o = dout("kc_o", [nl * 512, 512])
    vc_o = dout("vc_o", [nl * 512, 512])
    ka_o = dout("ka_o", [nl * 2 * 128, 128])
    va_o = dout("va_o", [nl * 128, 256])
    hl_o = dout("hl_o", [128, nl * 8])
    cl_o = dout("cl_o", [128, nl * 8 * 3])
    kcs_o = dout("kcs_o", [nl * 512, TS])
    vcs_o = dout("vcs_o", [nl * TS, 512])
    kas_o = dout("kas_o", [nl * 2 * 128, TS])
    vas_o = dout("vas_o", [nl * TS, 256])
    hls_o = dout("hls_o", [128, nl * 8])
    cls_o = dout("cls_o", [128, nl * 8 * 3])

    xs_t = nc.dram_tensor("xs_spill", [128, NCH * NT], F32)
    X1R = 1536
    xb1_t = nc.dram_tensor("xb1", [X1R, 512], BF16)
    yb1_t = nc.dram_tensor("yb1", [2 * X1R, 512], BF16)
    xb2_t = nc.dram_tensor("xb2", [128, 8], F32)
    yb2_t = nc.dram_tensor("yb2", [256, 8], F32)
    xb0_t = nc.dram_tensor("xb0", [128, nl * MCH * 12], F32)
    yb0_t = nc.dram_tensor("yb0", [ncores * 128, nl * MCH * 12], F32)
    pair_groups = [[2 * i, 2 * i + 1] for i in range(ncores // 2)]
    all_group = [list(range(ncores))]

    with contextlib.ExitStack() as st:
        k = K(nc, st)
        SZ_H = NCH * NT * 2
        SZ_X = NCH * NT * 4
        OFF_H = 0
        OFF_BIG = OFF_H + SZ_H
        OFF_X = OFF_BIG + SZ_X
        OFF_SLOT = OFF_X + SZ_X
        NSLOT = 3
        A = Arena(k, "arena", OFF_SLOT + NSLOT * 8192)
        hB = Buf(A, OFF_H, BF16, [NCH, NT])
        bigB = Buf(A, OFF_BIG, F32, [NCH, NT])
        xB = Buf(A, OFF_X, F32, [NCH, NT])
        slots = [Buf(A, OFF_SLOT + i * 8192, BF16, [4096]) for i in range(NSLOT)]
        o = OFF_BIG
        KC = Buf(A, o, BF16, [4, T]); o += 8192
        KhC = Buf(A, o, BF16, [4, 512]); o += 4096
        VC = Buf(A, o, BF16, [8, 512]); o += 8192
        VhC = Buf(A, o, BF16, [4, 512]); o += 4096
        KA = Buf(A, o, BF16, [2, T]); o += 4096
        KhA = Buf(A, o, BF16, [2, 128]); o += 512
        VA = Buf(A, o, BF16, [8, 256]); o += 4096
        VhA = Buf(A, o, BF16, [1, 256]); o += 512
        KsC = Buf(A, o, BF16, [4, 528]); o += 4352
        VsC = Buf(A, o, BF16, [5, 512]); o += 5120
        KsA = Buf(A, o, BF16, [2, 144]); o += 768
        VsA = Buf(A, o, BF16, [2, 256]); o += 1024
        stg = [Buf(A, o + i * 2048, F32, [512]) for i in range(2)]; o += 4096
        stg_s = Buf(A, o, F32, [256]); o += 1024
        xbuf = Buf(A, o, F32, [1032]); o += 4224
        lr_xc = Buf(A, o, F32, [512]); o += 2048
        lr_r = Buf(A, o, F32, [512]); o += 2048
        lr_i = Buf(A, o, F32, [512]); o += 2048
        lr_a = Buf(A, o, F32, [512]); o += 2048
        lr_h = Buf(A, o, F32, [512]); o += 2048
        assert o <= OFF_BIG + SZ_X, o - OFF_BIG
        o = OFF_X
        mixB = Buf(A, o, BF16, [NCH, NT]); o += SZ_H
        PgB = Buf(A, o, BF16, [8, T]); o += 16384
        QBD = Buf(A, o, BF16, [16, 128]); o += 4096
        biaspB = Buf(A, o, F32, [576]); o += 2304
        o_scr = o
        S_sb = [Buf(A, o + i * 2304, F32, [576]) for i in range(2)]; o += 4608
        P_sb = Buf(A, o, BF16, [640]); o += 1280
        PT_sb = [Buf(A, o + i * 1280, BF16, [5, 128]) for i in range(2)]; o += 2560
        assert o <= OFF_X + SZ_X, o - OFF_X
        lr_p = Buf(A, o_scr, F32, [512])
        lr_g = Buf(A, o_scr + 2048, BF16, [NT])
        lr_xcb = Buf(A, o_scr + 4608 + 1280, BF16, [512])
        uB = Buf(A, OFF_X, BF16, [8, NT])
        relu_s = [Buf(A, OFF_X + 16640 + i * 2048, F32, [512]) for i in range(2)]
        o = OFF_BIG
        modallB = Buf(A, o, F32, [nl * 96, 12]); o += nl * 96 * 12 * 4
        mstageB = Buf(A, o, F32, [nl * MCH * 12]); o += nl * MCH * 12 * 4
        modvB = Buf(A, o, F32, [nl * 96, 2]); o += nl * 96 * 2 * 4
        bmodB = Buf(A, o, F32, [nl * 96]); o += nl * 96 * 4
        gvB = Buf(A, o, F32, [nl * 4 * NCH]); o += nl * 4 * NCH * 4
        cTB = Buf(A, o, F32, [NCH * 12]); o += NCH * 12 * 4
        cTbB = Buf(A, o, BF16, [NCH * 12]); o += NCH * 12 * 2
        selB = Buf(A, o, F32, [24]); o += 96
        assert o <= OFF_BIG + SZ_X

        def pt(name, shape, dtype=F32):
            return k.sb(name, shape, dtype), Res(name)

        shv, R_shv = pt("shv", [128, nl * 2 * NCH * 2])
        gsv, R_gsv = pt("gsv", [128, nl * 2 * NCH * 2])
        ggv, R_ggv = pt("ggv", [128, nl * 2 * NCH * 2])
        lrup_s, R_lrup = pt("lrup_s", [128, nl * 64])
        nls_s, R_nls = pt("nls_s", [128, nl * 8 * 2])
        wri_s, R_wri = pt("wri_s", [128, 16 * 128], BF16)
        sink_s, R_sink = pt("sink_s", [128, nl * 4])
        sinks_s, R_sinks = pt("sinks_s", [32, nl * 4])
        flags_s, R_flags = pt("flags_s", [128, 2])
        ident_b, R_ident = pt("ident_b", [128, 128], BF16)
        ones_b, R_ones = pt("ones_b", [128, 128], BF16)
        cst, R_cst = pt("cst", [128, 4])
        rstd, R_rstd = pt("rstd", [128, NT])
        biass_s, R_biass = pt("biass_s", [32, 528])
        sth_s, R_sth = pt("sth_s", [128, nl * 8])
        stc_s, R_stc = pt("stc_s", [128, nl * 24])
        xbt, R_xbt = pt("xbt", [128, 24])
        pref, R_pref = pt("pref", [128, 24])
        hin, R_hin = pt("hin", [128, 8])
        h0l, R_h0l = pt("h0l", [128, 8])
        Pl, R_Pl = pt("Pl", [128, 8])
        hlt, R_hlt = pt("hlt", [128, 8])
        sm, R_sm = pt("sm", [128, 8])
        car, R_car = pt("car", [128, 2])
        sx, R_sx = pt("sx", [128, 20])
        sxc, R_sxc = pt("sxc", [128, TS])
        sxcb, R_sxcb = pt("sxcb", [128, TS], BF16)
        sr, R_sr = pt("sr", [128, TS])
        si, R_si = pt("si", [128, TS])
        sa, R_sa = pt("sa", [128, TS])
        sh, R_sh = pt("sh", [128, TS])
        hls_s, R_hls = pt("hls_s", [128, nl * 8])
        cls_s, R_cls = pt("cls_s", [128, nl * 24])
        hl_s, R_hl = pt("hl_s", [128, nl * 8])
        cl_s, R_cl = pt("cl_s", [128, nl * 24])
        QBDs, R_QBDs = pt("QBDs", [128, 32], BF16)

        banks = [k.ps(f"bank{i}", [128, 512]) for i in range(6)]
        R_bank = [Res(f"bank{i}") for i in range(6)]
        S2 = k.ps("S2", [128, 1024])
        R_S2 = Res("S2")
        PTb16 = banks[5][:, :].bitcast(BF16)

        R_xs = Res("xs_dram")
        R_xb1 = Res("xb1")
        R_yb1 = Res("yb1")
        R_xb2 = Res("xb2")
        R_yb2 = Res("yb2")
        R_xb0 = Res("xb0")
        R_yb0 = Res("yb0")
        R_out = Res("outs")
        cc_sem = k.newsem("cc")

        wq = []
        wstate = {"issued": 0, "used": 0}

        def w_issue_upto(n):
            while wstate["issued"] < min(n, len(wq)):
                i = wstate["issued"]
                sl = slots[i % NSLOT]
                src, shp = wq[i]
                dst = sl.ap.rearrange("p (c n) -> p c n", c=(16 if shp == "k16" else 8))
                k.dma("pool", lambda e, dst=dst, src=src: e.dma_start(out=dst, in_=src), reads=[], writes=sl.rs())
                wstate["issued"] += 1

        def w_next():
            i = wstate["used"]
            w_issue_upto(i + NSLOT)
            wstate["used"] += 1
            return slots[i % NSLOT]

        def wsrc_k16(dram, l, c0):
            return (dram[l].rearrange("(c p) n -> p c n", p=128)[:, :, c0:c0 + 256], "k16")

        def wsrc_dn(l, hb, mg):
            return (wdn_d[l, hb * 1024:(hb + 1) * 1024, mg * 512:(mg + 1) * 512].rearrange("(c p) n -> p c n", p=128), "k8")

        for l in range(nl):
            for b in range(MBLK):
                wq.append(wsrc_k16(wmod_d, l, b * 256))
        for l in range(nl):
            for b in range(22):
                wq.append(wsrc_k16(win_d, l, b * 256))
            for b in range(8):
                wq.append(wsrc_k16(wout_d, l, b * 256))
            for hb in range(8):
                for b in range(4):
                    wq.append(wsrc_k16(wup_d, l, hb * 1024 + b * 256))
                for mg in range(4):
                    wq.append(wsrc_dn(l, hb, mg))

        def act(fn, reads, writes):
            k.op("act", fn, reads, writes)

        def dve(fn, reads, writes):
            k.op("dve", fn, reads, writes)

        def pe(fn, reads, writes, inc=True):
            k.op("pe", fn, reads, writes, inc=inc)

        def sp_dma(out, in_, reads, writes):
            k.dma("sp", lambda e: e.dma_start(out=out, in_=in_), reads=reads, writes=writes)

        def pool_dma(out, in_, reads, writes):
            k.dma("pool", lambda e: e.dma_start(out=out, in_=in_), reads=reads, writes=writes)

        TILES = [(0, 512), (512, 512), (T, TS)]

        def gemm_fm(w3, col0, rhsB, nk, bset, wres):
            for kc in range(nk):
                lw = w3[:, kc, col0:col0 + 128]
                last = (kc == nk - 1)
                for ti, (t0, tn) in enumerate(TILES):
                    bk = bset[ti]
                    pe(lambda e, bk=bk, lw=lw, kc=kc, t0=t0, tn=tn, last=last:
                       e.matmul(banks[bk][:, 0:tn], lw, rhsB.ap[:, kc, t0:t0 + tn], start=(kc == 0), stop=last),
                       reads=wres + rhsB.rows(kc), writes=[R_bank[bk]], inc=(last and ti == 2))

        sp_dma(cTB.ap, cT_d, [], cTB.rs())
        sp_dma(bmodB.ap, bmod_d, [], bmodB.rs())
        sp_dma(selB.ap, sel_d, [], selB.rs())
        sp_dma(gvB.ap, gv_d, [], gvB.rs())
        sp_dma(lrup_s[:, :], lrup_d, [], [R_lrup])
        sp_dma(sink_s[:, :], sink_d, [], [R_sink])
        sp_dma(sinks_s[:, :], sinks_d, [], [R_sinks])
        sp_dma(flags_s[:, :], flags_d, [], [R_flags])
        sp_dma(sth_s[:, :], sth_d, [], [R_sth])
        sp_dma(stc_s[:, :], stc_d, [], [R_stc])
        pool_dma(ident_b[:, :], ident_d, [], [R_ident])
        sp_dma(xB.ap.rearrange("p c t -> p (c t)"), xT_d, [], xB.rs())
        dve(lambda e: e.memset(ones_b[:, :], 1.0), [], [R_ones])
        dve(lambda e: e.memset(cst[:, 0:1], 1e-6), [], [R_cst])
        dve(lambda e: e.memset(cst[:, 1:2], 1.0), [], [R_cst])
        dve(lambda e: e.memset(cst[:, 2:3], 0.0), [], [R_cst])
        dve(lambda e: e.memset(QBDs[:, :], 0.0), [], [R_QBDs])
        zeros_bc = cst[:, 2:3].to_broadcast([128, 512])
        lam_v = lrup_s[:, :].rearrange("p (n e) -> p n e", e=8)[:, :, 7]
        nls_v = nls_s[:, :].rearrange("p (n e) -> p n e", e=2)
        act(lambda e: e.activation(out=nls_v[:, :, 0], in_=lam_v, func=AF.Exp, scale=-1.0), [R_lrup], [R_nls])
        act(lambda e: e.activation(out=nls_v[:, :, 0], in_=nls_v[:, :, 0], func=AF.Ln, bias=cst[:, 1:2], scale=1.0), [R_cst, R_nls], [R_nls])
        dve(lambda e: e.tensor_scalar(out=nls_v[:, :, 1], in0=nls_v[:, :, 0], scalar1=-16.0, scalar2=None, op0=ALU.mult), [R_nls], [R_nls])
        dve(lambda e: e.tensor_scalar(out=nls_v[:, :, 0], in0=nls_v[:, :, 0], scalar1=-8.0, scalar2=None, op0=ALU.mult), [R_nls], [R_nls])
        act(lambda e: e.activation(out=cTbB.ap, in_=cTB.ap, func=AF.Silu), cTB.rs(), cTbB.rs())

        cTv = cTbB.ap.rearrange("p (c j) -> p c j", j=12)
        for l in range(nl):
            for b in range(MBLK):
                sl = w_next()
                w16 = sl.ap.rearrange("p (c n) -> p c n", c=16)
                for mm in range(2):
                    mc = b * 2 + mm
                    bk = mc % 2
                    for kc in range(16):
                        last = kc == 15
                        pe(lambda e, bk=bk, w16=w16, kc=kc, mm=mm, last=last:
                           e.matmul(banks[bk][:, 0:12], w16[:, kc, mm * 128:(mm + 1) * 128], cTv[:, kc, :], start=(kc == 0), stop=last),
                           reads=sl.rs() + cTbB.rs(), writes=[R_bank[bk]], inc=last)
                    o0 = (l * MCH + mc) * 12
                    dve(lambda e, bk=bk, o0=o0: e.tensor_copy(out=mstageB.ap[:, o0:o0 + 12], in_=banks[bk][:, 0:12]),
                        [], [R_bank[bk]] + mstageB.rs(o0, o0 + 12))
        sp_dma(xb0_t.ap(), mstageB.ap, mstageB.rs(), [R_xb0])
        k.custom("pool", lambda e: e.collective_compute("AllGather", ALU.bypass, replica_groups=all_group,
                                                        ins=[xb0_t.ap().opt()], outs=[yb0_t.ap().opt()]),
                 cc_sem, reads=[R_xb0], writes=[R_yb0])
        mav = modallB.ap.rearrange("p (l r m) j -> p l r m j", l=nl, r=ncores)
        for r in range(ncores):
            for l in range(nl):
                src = yb0_t.ap()[r * 128:(r + 1) * 128, l * MCH * 12:(l + 1) * MCH * 12].rearrange("p (m j) -> p m j", j=12)
                k.dma("sp", lambda e, r=r, l=l, src=src: e.dma_start(out=mav[:, l, r, :, :], in_=src), reads=[R_yb0], writes=modallB.rs())
        for w in range(2):
            for j in range(12):
                if j == 0:
                    dve(lambda e, w=w: e.tensor_scalar(out=modvB.ap[:, :, w], in0=modallB.ap[:, :, 0], scalar1=selB.ap[:, w * 12:w * 12 + 1],
                                                       scalar2=None, op0=ALU.mult), modallB.rs() + selB.rs(), modvB.rs())
                else:
                    dve(lambda e, w=w, j=j: e.scalar_tensor_tensor(out=modvB.ap[:, :, w], in0=modallB.ap[:, :, j], scalar=selB.ap[:, w * 12 + j:w * 12 + j + 1],
                                                                   in1=modvB.ap[:, :, w], op0=ALU.mult, op1=ALU.add),
                        modallB.rs() + selB.rs() + modvB.rs(), modvB.rs())
            dve(lambda e, w=w: e.tensor_tensor(out=modvB.ap[:, :, w], in0=modvB.ap[:, :, w], in1=bmodB.ap, op=ALU.add), modvB.rs() + bmodB.rs(), modvB.rs())
        mv4 = modvB.ap.rearrange("p (l j c) w -> p l j c w", l=nl, j=6)
        gv4 = gvB.ap.rearrange("p (l q c) -> p l q c", l=nl, q=4)
        sh4 = shv[:, :].rearrange("p (l q c w) -> p l q c w", l=nl, q=2, c=NCH)
        gs4 = gsv[:, :].rearrange("p (l q c w) -> p l q c w", l=nl, q=2, c=NCH)
        gg4 = ggv[:, :].rearrange("p (l q c w) -> p l q c w", l=nl, q=2, c=NCH)
        for l in range(nl):
            for q in range(2):
                for w in range(2):
                    dve(lambda e, l=l, q=q, w=w: e.tensor_copy(out=sh4[:, l, q, :, w], in_=mv4[:, l, 3 * q, :, w]), modvB.rs(), [R_shv])
                    dve(lambda e, l=l, q=q, w=w: e.scalar_tensor_tensor(
                        out=gs4[:, l, q, :, w], in0=mv4[:, l, 1 + 3 * q, :, w], scalar=1.0, in1=gv4[:, l, 2 * q, :], op0=ALU.add, op1=ALU.mult),
                        modvB.rs() + gvB.rs(), [R_gsv])
                    dve(lambda e, l=l, q=q, w=w: e.tensor_tensor(
                        out=gg4[:, l, q, :, w], in0=mv4[:, l, 2 + 3 * q, :, w], in1=gv4[:, l, 2 * q + 1, :], op=ALU.mult), modvB.rs() + gvB.rs(), [R_ggv])

        def sumsq(srcB):
            for c in range(NCH):
                act(lambda e, c=c: e.activation(out=hB.ap[:, c, :], in_=srcB.ap[:, c, :], func=AF.Square), srcB.rows(c), hB.rows(c))
                last = (c == NCH - 1)
                for ti, (t0, tn) in enumerate(TILES):
                    pe(lambda e, ti=ti, t0=t0, tn=tn, c=c, last=last:
                       e.matmul(banks[ti][:, 0:tn], ones_b[:, :], hB.ap[:, c, t0:t0 + tn], start=(c == 0), stop=last),
                       reads=hB.rows(c) + [R_ones], writes=[R_bank[ti]], inc=(ti == 2))
            for ti, (t0, tn) in enumerate(TILES):
                act(lambda e, ti=ti, t0=t0, tn=tn: e.activation(out=rstd[:, t0:t0 + tn], in_=banks[ti][:, 0:tn], func=AF.Sqrt,
                                                                bias=cst[:, 0:1], scale=1.0 / D), [R_cst], [R_bank[ti], R_rstd])
            dve(lambda e: e.reciprocal(out=rstd[:, :], in_=rstd[:, :]), [R_rstd], [R_rstd])

        def norm_mod(l, q):
            sumsq(xB)
            for c in range(NCH):
                dve(lambda e, c=c: e.tensor_tensor(out=bigB.ap[:, c, :], in0=xB.ap[:, c, :], in1=rstd[:, :], op=ALU.mult),
                    xB.rows(c) + [R_rstd], bigB.rows(c))
                for w, (t0, tn) in enumerate([(0, T), (T, TS)]):
                    act(lambda e, c=c, w=w, t0=t0, tn=tn: e.activation(
                        out=hB.ap[:, c, t0:t0 + tn], in_=bigB.ap[:, c, t0:t0 + tn], func=AF.Identity,
                        bias=sh4[:, l, q, c, w:w + 1], scale=gs4[:, l, q, c, w:w + 1]),
                        bigB.rows(c) + [R_shv, R_gsv], hB.rows(c))

        def post_norm_add(l, q):
            sumsq(bigB)
            for c in range(NCH):
                dve(lambda e, c=c: e.tensor_tensor(out=bigB.ap[:, c, :], in0=bigB.ap[:, c, :], in1=rstd[:, :], op=ALU.mult),
                    bigB.rows(c) + [R_rstd], bigB.rows(c))
                for w, (t0, tn) in enumerate([(0, T), (T, TS)]):
                    dve(lambda e, c=c, w=w, t0=t0, tn=tn: e.scalar_tensor_tensor(
                        out=xB.ap[:, c, t0:t0 + tn], in0=bigB.ap[:, c, t0:t0 + tn], scalar=gg4[:, l, q, c, w:w + 1],
                        in1=xB.ap[:, c, t0:t0 + tn], op0=ALU.mult, op1=ALU.add),
                        bigB.rows(c) + [R_ggv] + xB.rows(c), xB.rows(c))

        def spill_x():
            sp_dma(xs_t.ap(), xB.ap.rearrange("p c t -> p (c t)"), xB.rs(), [R_xs])

        def reload_x():
            sp_dma(xB.ap.rearrange("p c t -> p (c t)"), xs_t.ap(), [R_xs], xB.rs())

        def attend(qbd_ap, qbd_res, nq2, nqh, segs, bias_ap, bias_res, sink_col, sink_res, S_ap, S_res, Pfull, P_res, tcol0, PT_ap, PT_res,
                   vblocks, out_ap, out_res):
            col = 0
            pieces = []
            for (kap, kres, masked, n) in segs:
                c0 = 0
                while c0 < n:
                    cn = min(n - c0, 512 - (col % 512))
                    pieces.append((kap[:, c0:c0 + cn], kres, col, cn))
                    col += cn
                    c0 += cn
            ntot = col
            for (kap, kres, c0, cn) in pieces:
                pe(lambda e, kap=kap, c0=c0, cn=cn: e.matmul(S2[0:nq2, c0:c0 + cn], qbd_ap, kap, start=True, stop=True),
                   qbd_res + kres, [R_S2])
            col = 0
            for (kap, kres, masked, n) in segs:
                sl_ps = S2[0:nq2, col:col + n]
                sl_sb = S_ap[:, col:col + n]
                if bias_ap is not None and masked:
                    dve(lambda e, sl_ps=sl_ps, sl_sb=sl_sb, col=col, n=n: e.scalar_tensor_tensor(
                        out=sl_sb, in0=sl_ps, scalar=flags_s[0:nq2, 0:1], in1=bias_ap[:, col:col + n], op0=ALU.add, op1=ALU.add),
                        [R_flags] + bias_res, [R_S2] + S_res)
                elif bias_ap is not None:
                    dve(lambda e, sl_ps=sl_ps, sl_sb=sl_sb, col=col, n=n: e.tensor_tensor(
                        out=sl_sb, in0=sl_ps, in1=bias_ap[:, col:col + n], op=ALU.add), bias_res, [R_S2] + S_res)
                elif masked:
                    dve(lambda e, sl_ps=sl_ps, sl_sb=sl_sb: e.tensor_scalar(
                        out=sl_sb, in0=sl_ps, scalar1=flags_s[0:nq2, 0:1], scalar2=None, op0=ALU.add), [R_flags], [R_S2] + S_res)
                else:
                    dve(lambda e, sl_ps=sl_ps, sl_sb=sl_sb: e.tensor_copy(out=sl_sb, in_=sl_ps), [], [R_S2] + S_res)
                col += n
            Sv = S_ap[:, 0:ntot]
            Pv = Pfull[:, 64:64 + ntot]
            dve(lambda e: e.tensor_reduce(out=sm[0:nq2, 2:3], in_=Sv, axis=AX.X, op=ALU.max, negate=True), S_res, [R_sm])
            act(lambda e: e.activation(out=Pv, in_=Sv, func=AF.Exp, bias=sm[0:nq2, 2:3], scale=1.0, accum_out=sm[0:nq2, 3:4]),
                S_res + [R_sm], P_res + [R_sm])
            if sink_col is not None:
                act(lambda e: e.activation(out=sm[0:nq2, 4:5], in_=sink_col, func=AF.Exp, bias=sm[0:nq2, 2:3], scale=1.0), [R_sm] + sink_res, [R_sm])
                dve(lambda e: e.tensor_tensor(out=sm[0:nq2, 3:4], in0=sm[0:nq2, 3:4], in1=sm[0:nq2, 4:5], op=ALU.add), [R_sm], [R_sm])
            dve(lambda e: e.reciprocal(out=sm[0:nq2, 5:6], in_=sm[0:nq2, 3:4]), [R_sm], [R_sm])
            dve(lambda e: e.tensor_scalar(out=Pv, in0=Pv, scalar1=sm[0:nq2, 5:6], scalar2=None, op0=ALU.mult), P_res + [R_sm], P_res)
            col = tcol0
            nb = len(vblocks)
            for bi, (vap, vres, nk) in enumerate(vblocks):
                pe(lambda e, col=col, nk=nk, bi=bi: e.transpose(out=PTb16[0:nk, bi * 128:bi * 128 + nq2], in_=Pfull[:, col:col + nk],
                                                               identity=ident_b[0:nq2, 0:nq2]),
                   P_res + [R_ident], [R_bank[5]], inc=(bi == nb - 1))
                col += nk
            assert col == 64 + ntot, (col, ntot, tcol0)
            act(lambda e: e.activation(out=PT_ap[:, 0:nb, 0:nq2], in_=PTb16[:, 0:nb * 128].rearrange("p (b q) -> p b q", q=128)[:, :, 0:nq2], func=AF.Identity),
                [], [R_bank[5]] + PT_res)
            for bi, (vap, vres, nk) in enumerate(vblocks):
                pe(lambda e, vap=vap, nk=nk, bi=bi: e.matmul(banks[4][:, 0:nq2], vap, PT_ap[0:nk, bi, 0:nq2], start=(bi == 0), stop=(bi == nb - 1)),
                   vres + PT_res, [R_bank[4]], inc=(bi == nb - 1))
            dve(lambda e: e.tensor_copy(out=out_ap[0:64, :], in_=banks[4][0:64, 0:nqh]), [], [R_bank[4]] + out_res)
            dve(lambda e: e.tensor_copy(out=out_ap[64:128, :], in_=banks[4][64:128, nqh:2 * nqh]), [], [R_bank[4]] + out_res)

        def prompt_window(c, nprev, KX, KhX, kidx, halo_len, vfn):
            s = (c - nprev) * 64
            e_ = (c + 1) * 64
            segs = []
            if s < 0:
                segs.append((KhX.ap[:, kidx, halo_len + s:halo_len], KhX.rows(kidx), True, -s))
            lo = max(s, 0)
            segs.append((KX.ap[:, kidx, lo:e_], KX.rows(kidx), False, e_ - lo))
            if s % 128 == 0:
                tcol0, t0 = 64, s
            else:
                tcol0, t0 = 0, s - 64
            vblocks = []
            while t0 < e_:
                nk = min(128, e_ - t0)
                vblocks.append(vfn(t0, nk))
                t0 += 128
            return segs, vblocks, tcol0

        lr4 = lrup_s[:, :].rearrange("p (l c e) -> p l c e", l=nl, c=8)
        nls4 = nls_s[:, :].rearrange("p (l c e) -> p l c e", l=nl, c=8)
        wri4 = wri_s[:, :].rearrange("p (c g m) -> p c g m", c=8, g=2)
        sink3 = sink_s[:, :].rearrange("p (l q) -> p l q", l=nl)
        sinks3 = sinks_s[:, :].rearrange("p (l q) -> p l q", l=nl)
        pref3 = pref[:, :].rearrange("p (c t) -> p c t", t=3)
        xbt3 = xbt[:, :].rearrange("p (c t) -> p c t", t=3)
        stc4 = stc_s[:, :].rearrange("p (l c t) -> p l c t", l=nl, c=8)
        sth3 = sth_s[:, :].rearrange("p (l c) -> p l c", l=nl)
        flip = [0]
        x1 = xb1_t.ap()
        y1 = yb1_t.ap()

        norm_mod(0, 0)
        for l in range(nl):
            spill_x()
            dve(lambda e: e.memset(QBD.ap, 0.0), [], QBD.rs())
            dve(lambda e: e.memset(P_sb.ap[:, 0:64], 0.0), [], P_sb.rs(0, 64))
            for pr in range(4):
                pool_dma(KsC.ap[:, pr, 0:512], cbk_d[(l * 4 + pr) * 128:(l * 4 + pr + 1) * 128, :], [], KsC.rows(pr))
                pool_dma(VsC.ap[:, pr, :], cbv_d[l * 512 + pr * 128:l * 512 + (pr + 1) * 128, :], [], VsC.rows(pr))
            for kh in range(2):
                pool_dma(KsA.ap[:, kh, 0:128], cswak_d[(l * 2 + kh) * 128:(l * 2 + kh + 1) * 128, :], [], KsA.rows(kh))
            pool_dma(VsA.ap[:, 0, :], cswav_d[l * 128:(l + 1) * 128, :], [], VsA.rows(0))
            pool_dma(wri_s[:, :].rearrange("p (n m) -> p n m", m=128),
                     wri_d[l * 2048:(l + 1) * 2048, :].rearrange("(n kk) m -> kk n m", kk=128), [], [R_wri])

            for b in range(4):
                sl = w_next()
                w16 = sl.ap.rearrange("p (c n) -> p c n", c=16)
                for mm in range(2):
                    c = b * 2 + mm
                    bk = 3 + (c % 2)
                    for kc in range(16):
                        last = kc == 15
                        pe(lambda e, bk=bk, w16=w16, kc=kc, mm=mm, last=last:
                           e.matmul(banks[bk][:, 0:3], w16[:, kc, mm * 128:(mm + 1) * 128], hB.ap[:, kc, T - 3:T], start=(kc == 0), stop=last),
                           reads=sl.rs() + hB.rows(kc), writes=[R_bank[bk]], inc=last)
                    dve(lambda e, bk=bk, c=c: e.tensor_copy(out=xbt3[:, c, :], in_=banks[bk][:, 0:3]), [], [R_bank[bk], R_xbt])
            sp_dma(x1[1408:1536, 0:48], xbt[:, :].bitcast(BF16), [R_xbt], [R_xb1])
            dve(lambda e, l=l: e.tensor_copy(out=cl_s[:, l * 24:(l + 1) * 24], in_=xbt[:, :]), [R_xbt], [R_cl])

            def k_chunk(sl, mm, dstB, dsti, halo_row0, out_d, out_r0, tail0, s_dst, s_dst_res, s_out, s_r0):
                w16 = sl.ap.rearrange("p (c n) -> p c n", c=16)
                gemm_fm(w16, mm * 128, hB, 16, (0, 1, 2), sl.rs())
                act(lambda e: e.activation(out=dstB.ap[:, dsti, 0:512], in_=banks[0][:, :], func=AF.Identity),
                    [], [R_bank[0]] + dstB.rs(dsti * T, dsti * T + 512))
                sg = stg[flip[0] % 2]
                flip[0] += 1
                dve(lambda e: e.tensor_copy(out=dstB.ap[:, dsti, 512:T], in_=banks[1][:, :]),
                    [], [R_bank[1]] + dstB.rs(dsti * T + 512, dsti * T + T))
                act(lambda e: e.activation(out=sg.ap, in_=banks[1][:, :], func=AF.Identity), [], [R_bank[1]] + sg.rs())
                ntail = 512 - tail0
                sp_dma(out_d[out_r0:out_r0 + 128, 0:ntail], sg.ap[:, tail0:512], sg.rs(), [R_out])
                sp_dma(x1[halo_row0:halo_row0 + 128, 0:ntail], dstB.ap[:, dsti, 512 + tail0:T], dstB.rs(dsti * T + 512, dsti * T + T), [R_xb1])
                dve(lambda e: e.tensor_copy(out=stg_s.ap[:, 0:TS], in_=banks[2][:, 0:TS]), [], [R_bank[2]] + stg_s.rs())
                dve(lambda e: e.tensor_copy(out=s_dst, in_=stg_s.ap[:, 0:TS]), stg_s.rs(), s_dst_res)
                sp_dma(s_out[s_r0:s_r0 + 128, :], stg_s.ap[:, 0:TS], stg_s.rs(), [R_out])

            for b in range(2):
                sl = w_next()
                for mm in range(2):
                    pr = b * 2 + mm
                    k_chunk(sl, mm, KC, pr, pr * 128, kc_o, (l * 4 + pr) * 128, 0, KsC.ap[:, pr, 512:528], KsC.rows(pr), kcs_o, (l * 4 + pr) * 128)
            sl = w_next()
            for mm in range(2):
                k_chunk(sl, mm, KA, mm, 1024 + mm * 128, ka_o, (l * 2 + mm) * 128, 384, KsA.ap[:, mm, 128:144], KsA.rows(mm), kas_o, (l * 2 + mm) * 128)

            def v_block(sl, dstB, col0, out_d, out_r0, halo_r0, tail_tb, s_dstB, s_blk, s_out, s_r0):
                ncol = 256
                w16 = sl.ap.rearrange("p (c n) -> p c n", c=16)
                for tb in range(9):
                    bk = 3 + (tb % 2)
                    m = 128 if tb < 8 else TS
                    t0 = tb * 128
                    for kc in range(16):
                        last = kc == 15
                        pe(lambda e, bk=bk, w16=w16, kc=kc, t0=t0, m=m, last=last:
                           e.matmul(banks[bk][0:m, 0:ncol], hB.ap[:, kc, t0:t0 + m], w16[:, kc, 0:ncol], start=(kc == 0), stop=last),
                           reads=sl.rs() + hB.rows(kc), writes=[R_bank[bk]], inc=last)
                    if tb < 8:
                        act(lambda e, bk=bk, tb=tb: e.activation(out=dstB.ap[:, tb, col0:col0 + ncol], in_=banks[bk][:, 0:ncol], func=AF.Identity),
                            [], [R_bank[bk]] + dstB.rows(tb))
                        if tb >= tail_tb:
                            sg = stg[flip[0] % 2]
                            flip[0] += 1
                            dve(lambda e, bk=bk, sg=sg: e.tensor_copy(out=sg.ap[:, 0:ncol], in_=banks[bk][:, 0:ncol]), [], [R_bank[bk]] + sg.rs())
                            r = (tb - tail_tb) * 128
                            sp_dma(out_d[out_r0 + r:out_r0 + r + 128, col0:col0 + ncol], sg.ap[:, 0:ncol], sg.rs(), [R_out])
                            sp_dma(x1[halo_r0 + r:halo_r0 + r + 128, col0:col0 + ncol], dstB.ap[:, tb, col0:col0 + ncol], dstB.rows(tb), [R_xb1])
                    else:
                        dve(lambda e, bk=bk: e.tensor_copy(out=stg_s.ap[0:TS, 0:ncol], in_=banks[bk][0:TS, 0:ncol]), [], [R_bank[bk]] + stg_s.rs())
                        dve(lambda e: e.tensor_copy(out=s_dstB.ap[0:TS, s_blk, col0:col0 + ncol], in_=stg_s.ap[0:TS, 0:ncol]),
                            stg_s.rs(), s_dstB.rows(s_blk))
                        sp_dma(s_out[s_r0:s_r0 + TS, col0:col0 + ncol], stg_s.ap[0:TS, 0:ncol], stg_s.rs(), [R_out])

            for b in range(2):
                sl = w_next()
                v_block(sl, VC, b * 256, vc_o, l * 512, 512, 4, VsC, 4, vcs_o, l * TS)
            sl = w_next()
            v_block(sl, VA, 0, va_o, l * 128, 1280, 7, VsA, 1, vas_o, l * TS)

            k.custom("pool", lambda e: e.collective_compute("AllGather", ALU.bypass, replica_groups=pair_groups,
                                                            ins=[xb1_t.ap().opt()], outs=[yb1_t.ap().opt()]),
                     cc_sem, reads=[R_xb1], writes=[R_yb1])
            for pr in range(4):
                sp_dma(KhC.ap[:, pr, :], y1[pr * 128:(pr + 1) * 128, :], [R_yb1], KhC.rows(pr))
                sp_dma(VhC.ap[:, pr, :], y1[512 + pr * 128:512 + (pr + 1) * 128, :], [R_yb1], VhC.rows(pr))
            for kh in range(2):
                sp_dma(KhA.ap[:, kh, :], y1[1024 + kh * 128:1024 + (kh + 1) * 128, 0:128], [R_yb1], KhA.rows(kh))
            sp_dma(VhA.ap[:, 0, :], y1[1280:1408, 0:256], [R_yb1], VhA.rows(0))
            sp_dma(pref[:, :].bitcast(BF16), y1[1408:1536, 0:48], [R_yb1], [R_pref])
            dve(lambda e: e.tensor_scalar(out=pref[:, :], in0=pref[:, :], scalar1=flags_s[:, 1:2], scalar2=None, op0=ALU.mult),
                [R_pref, R_flags], [R_pref])

            def lru_core(xsrc, xsrc_res, n, cw, ns, wr, wi, xc_ap, xc_res, xcb_ap, xcb_res, r_ap, r_res, i_ap, i_res, a_ap, a_res, bkr, bki):
                dve(lambda e: e.tensor_scalar(out=xc_ap, in0=xsrc[:, 0:n], scalar1=cw[:, 0:1], scalar2=cw[:, 4:5], op0=ALU.mult, op1=ALU.add),
                    xsrc_res + [R_lrup], xc_res)
                for j in range(1, 4):
                    dve(lambda e, j=j: e.scalar_tensor_tensor(out=xc_ap, in0=xsrc[:, j:j + n], scalar=cw[:, j:j + 1], in1=xc_ap, op0=ALU.mult, op1=ALU.add),
                        xsrc_res + [R_lrup] + xc_res, xc_res)
                act(lambda e: e.activation(out=xcb_ap, in_=xc_ap, func=AF.Identity), xc_res, xcb_res)
                pe(lambda e: e.matmul(banks[bkr][:, 0:n], wr, xcb_ap, start=True, stop=True), [R_wri] + xcb_res, [R_bank[bkr]])
                pe(lambda e: e.matmul(banks[bki][:, 0:n], wi, xcb_ap, start=True, stop=True), [R_wri] + xcb_res, [R_bank[bki]])
                act(lambda e: e.activation(out=r_ap, in_=banks[bkr][:, 0:n], func=AF.Sigmoid, bias=cw[:, 5:6], scale=1.0), [R_lrup], [R_bank[bkr]] + r_res)
                act(lambda e: e.activation(out=i_ap, in_=banks[bki][:, 0:n], func=AF.Sigmoid, bias=cw[:, 6:7], scale=1.0), [R_lrup], [R_bank[bki]] + i_res)
                act(lambda e: e.activation(out=a_ap, in_=r_ap, func=AF.Exp, scale=ns[:, 0:1]), r_res + [R_nls], a_res)
                act(lambda e: e.activation(out=r_ap, in_=r_ap, func=AF.Exp, scale=ns[:, 1:2]), r_res + [R_nls], r_res)
                act(lambda e: e.activation(out=r_ap, in_=r_ap, func=AF.Sqrt, bias=cst[:, 1:2], scale=-1.0), r_res + [R_cst], r_res)
                dve(lambda e: e.tensor_tensor(out=i_ap, in0=i_ap, in1=r_ap, op=ALU.mult), i_res + r_res, i_res)
                dve(lambda e: e.tensor_tensor(out=i_ap, in0=i_ap, in1=xc_ap, op=ALU.mult), i_res + xc_res, i_res)

            for c in range(8):
                sl = w_next()
                w16 = sl.ap.rearrange("p (c n) -> p c n", c=16)
                gemm_fm(w16, 0, hB, 16, (0, 1, 2), sl.rs())
                gemm_fm(w16, 128, hB, 16, (3, 4, 5), sl.rs())
                dve(lambda e, c=c: e.tensor_copy(out=xbuf.ap[:, 0:3], in_=pref3[:, c, :]), [R_pref], xbuf.rs(0, 3))
                act(lambda e: e.activation(out=xbuf.ap[:, 3:515], in_=banks[0][:, :], func=AF.Identity), [], [R_bank[0]] + xbuf.rs(3, 515))
                dve(lambda e: e.tensor_copy(out=xbuf.ap[:, 515:1027], in_=banks[1][:, :]), [], [R_bank[1]] + xbuf.rs(515, 1027))
                dve(lambda e, c=c, l=l: e.tensor_copy(out=sx[:, 0:3], in_=stc4[:, l, c, :]), [R_stc], [R_sx])
                dve(lambda e: e.tensor_copy(out=sx[:, 3:19], in_=banks[2][:, 0:TS]), [], [R_bank[2], R_sx])
                for (bkg, t_ap, t_res, g0, gn) in ((3, lr_r.ap, lr_r.rs(), 0, 512), (4, lr_i.ap, lr_i.rs(), 512, 512), (5, sr[:, :], [R_sr], T, TS)):
                    act(lambda e, bkg=bkg, t_ap=t_ap, gn=gn: e.activation(out=t_ap, in_=banks[bkg][:, 0:gn], func=AF.Square), [], [R_bank[bkg]] + t_res)
                    dve(lambda e, t_ap=t_ap: e.tensor_scalar(out=t_ap, in0=t_ap, scalar1=0.044715, scalar2=1.0, op0=ALU.mult, op1=ALU.add), t_res, t_res)
                    dve(lambda e, bkg=bkg, t_ap=t_ap, gn=gn: e.tensor_tensor(out=t_ap, in0=t_ap, in1=banks[bkg][:, 0:gn], op=ALU.mult), t_res, [R_bank[bkg]] + t_res)
                    act(lambda e, t_ap=t_ap: e.activation(out=t_ap, in_=t_ap, func=AF.Sigmoid, scale=1.5957691216), t_res, t_res)
                    dve(lambda e, bkg=bkg, t_ap=t_ap, g0=g0, gn=gn: e.tensor_tensor(out=lr_g.ap[:, g0:g0 + gn], in0=t_ap, in1=banks[bkg][:, 0:gn], op=ALU.mult),
                        t_res, [R_bank[bkg]] + lr_g.rs(g0, g0 + gn))
                cw = lr4[:, l, c, :]
                ns = nls4[:, l, c, :]
                wr = wri4[:, c, 0, :]
                wi = wri4[:, c, 1, :]
                for ti in range(2):
                    t0 = ti * 512
                    lru_core(xbuf.ap[:, t0:t0 + 515], xbuf.rs(t0, t0 + 515), 512, cw, ns, wr, wi, lr_xc.ap, lr_xc.rs(), lr_xcb.ap, lr_xcb.rs(),
                             lr_r.ap, lr_r.rs(), lr_i.ap, lr_i.rs(), lr_a.ap, lr_a.rs(), 0 + ti, 2 + ti)
                    ini_h = 0.0 if ti == 0 else car[:, 0:1]
                    ini_p = 1.0 if ti == 0 else car[:, 1:2]
                    dve(lambda e, ini_h=ini_h: e.tensor_tensor_scan(out=lr_h.ap, data0=lr_a.ap, data1=lr_i.ap, initial=ini_h, op0=ALU.mult, op1=ALU.add),
                        lr_a.rs() + lr_i.rs() + [R_car], lr_h.rs())
                    dve(lambda e, ini_p=ini_p: e.tensor_tensor_scan(out=lr_p.ap, data0=lr_a.ap, data1=zeros_bc, initial=ini_p, op0=ALU.mult, op1=ALU.add),
                        lr_a.rs() + [R_cst, R_car], lr_p.rs())
                    dve(lambda e: e.tensor_copy(out=car[:, 0:1], in_=lr_h.ap[:, 511:512]), lr_h.rs(), [R_car])
                    dve(lambda e: e.tensor_copy(out=car[:, 1:2], in_=lr_p.ap[:, 511:512]), lr_p.rs(), [R_car])
                    dve(lambda e, c=c, t0=t0: e.tensor_tensor(out=mixB.ap[:, 4 + c, t0:t0 + 512], in0=lr_h.ap, in1=lr_g.ap[:, t0:t0 + 512], op=ALU.mult),
                        lr_h.rs() + lr_g.rs(t0, t0 + 512), mixB.rs((4 + c) * NT + t0, (4 + c) * NT + t0 + 512))
                    dve(lambda e, c=c, t0=t0: e.tensor_tensor(out=PgB.ap[:, c, t0:t0 + 512], in0=lr_p.ap, in1=lr_g.ap[:, t0:t0 + 512], op=ALU.mult),
                        lr_p.rs() + lr_g.rs(t0, t0 + 512), PgB.rs(c * T + t0, c * T + t0 + 512))
                dve(lambda e, c=c: e.tensor_copy(out=h0l[:, c:c + 1], in_=car[:, 0:1]), [R_car], [R_h0l])
                dve(lambda e, c=c: e.tensor_copy(out=Pl[:, c:c + 1], in_=car[:, 1:2]), [R_car], [R_Pl])
                lru_core(sx[:, 0:19], [R_sx], TS, cw, ns, wr, wi, sxc[:, :], [R_sxc], sxcb[:, :], [R_sxcb], sr[:, :], [R_sr], si[:, :], [R_si], sa[:, :], [R_sa], 0, 2)
                dve(lambda e, c=c, l=l: e.tensor_tensor_scan(out=sh[:, :], data0=sa[:, :], data1=si[:, :], initial=sth3[:, l, c:c + 1], op0=ALU.mult, op1=ALU.add),
                    [R_sa, R_si, R_sth], [R_sh])
                dve(lambda e, c=c: e.tensor_tensor(out=mixB.ap[:, 4 + c, T:NT], in0=sh[:, :], in1=lr_g.ap[:, T:NT], op=ALU.mult),
                    [R_sh] + lr_g.rs(T, NT), mixB.rs((4 + c) * NT + T, (4 + c + 1) * NT))
                dve(lambda e, c=c, l=l: e.tensor_copy(out=hls_s[:, l * 8 + c:l * 8 + c + 1], in_=sh[:, TS - 1:TS]), [R_sh], [R_hls])
                dve(lambda e, c=c, l=l: e.tensor_copy(out=cls_s[:, (l * 8 + c) * 3:(l * 8 + c) * 3 + 3], in_=sx[:, 16:19]), [R_sx], [R_cls])

            sp_dma(xb2_t.ap(), h0l[:, :], [R_h0l], [R_xb2])
            k.custom("pool", lambda e: e.collective_compute("AllGather", ALU.bypass, replica_groups=pair_groups,
                                                            ins=[xb2_t.ap().opt()], outs=[yb2_t.ap().opt()]),
                     cc_sem, reads=[R_xb2], writes=[R_yb2])
            sp_dma(hin[:, :], yb2_t.ap()[0:128, :], [R_yb2], [R_hin])
            dve(lambda e: e.tensor_scalar(out=hin[:, :], in0=hin[:, :], scalar1=flags_s[:, 1:2], scalar2=None, op0=ALU.mult), [R_hin, R_flags], [R_hin])

            for qb in range(4):
                sl = w_next()
                w16 = sl.ap.rearrange("p (c n) -> p c n", c=16)
                for mm in range(2):
                    qc = (qb % 2) * 2 + mm
                    is_band = qb >= 2
                    gemm_fm(w16, mm * 128, hB, 16, (0, 1, 2), sl.rs())
                    q4 = QBD.ap.rearrange("p c (h q) -> p c h q", h=2)
                    for ti in range(2):
                        for hh in range(2):
                            act(lambda e, ti=ti, hh=hh, q4=q4: e.activation(
                                out=q4[64 * hh:64 * hh + 64, ti * 8:(ti + 1) * 8, hh, :],
                                in_=banks[ti][64 * hh:64 * hh + 64, :].rearrange("p (c q) -> p c q", q=64),
                                func=AF.Identity, scale=0.125), [], [R_bank[ti]] + QBD.rows(ti * 8, ti * 8 + 8))
                    qs4 = QBDs[:, :].rearrange("p (h q) -> p h q", h=2)
                    for hh in range(2):
                        act(lambda e, hh=hh: e.activation(out=qs4[64 * hh:64 * hh + 64, hh, :], in_=banks[2][64 * hh:64 * hh + 64, 0:TS],
                                                          func=AF.Identity, scale=0.125), [], [R_bank[2], R_QBDs])
                    Ss_ap = S_sb[0].ap[0:32, 0:528]
                    Ps_ap = P_sb.ap[0:32, :]
                    PTs_ap = PT_sb[0].ap[:, :, 0:32]
                    if is_band:
                        pr = qc
                        r0 = (l * 4 + pr) * 128
                        sp_dma(biaspB.ap, biasp_d[r0:r0 + 128, :], [], biaspB.rs())
                        r0s = (l * 4 + pr) * 32
                        sp_dma(biass_s[:, :], biass_d[r0s:r0s + 32, :], [], [R_biass])
                        mchunk = 12 + pr

                        def vfn(t0, nk, pr=pr):
                            gb = (t0 + 512) // 128
                            if gb < 4:
                                return (VhC.ap[0:nk, gb, pr * 128:(pr + 1) * 128], VhC.rows(gb), nk)
                            return (VC.ap[0:nk, gb - 4, pr * 128:(pr + 1) * 128], VC.rows(gb - 4), nk)

                        for c in range(16):
                            segs, vblocks, tcol0 = prompt_window(c, 8, KC, KhC, pr, 512, vfn)
                            sb_i = c % 2
                            attend(QBD.ap[:, c, :], QBD.rows(c), 128, 64, segs, biaspB.ap, biaspB.rs(), None, [],
                                   S_sb[sb_i].ap, S_sb[sb_i].rs(), P_sb.ap, P_sb.rs(), tcol0, PT_sb[sb_i].ap, PT_sb[sb_i].rs(), vblocks,
                                   mixB.ap[:, mchunk, c * 64:(c + 1) * 64], mixB.rs(mchunk * NT + c * 64, mchunk * NT + (c + 1) * 64))
                        segs = [(KsC.ap[:, pr, 0:528], KsC.rows(pr), False, 528)]
                        vblocks = [(VsC.ap[:, b4, pr * 128:(pr + 1) * 128], VsC.rows(b4), 128) for b4 in range(4)]
                        vblocks.append((VsC.ap[0:TS, 4, pr * 128:(pr + 1) * 128], VsC.rows(4), TS))
                        attend(QBDs[:, :], [R_QBDs], 32, TS, segs, biass_s[:, :], [R_biass], None, [],
                               Ss_ap, S_sb[0].rs(), Ps_ap, P_sb.rs(), 64, PTs_ap, PT_sb[0].rs(), vblocks,
                               mixB.ap[:, mchunk, T:NT], mixB.rs(mchunk * NT + T, (mchunk + 1) * NT))
                    else:
                        kh = qc // 2
                        mchunk = qc

                        def vfn(t0, nk, kh=kh):
                            if t0 < 0:
                                return (VhA.ap[0:nk, 0, kh * 128:(kh + 1) * 128], VhA.rows(0), nk)
                            return (VA.ap[0:nk, t0 // 128, kh * 128:(kh + 1) * 128], VA.rows(t0 // 128), nk)

                        for c in range(16):
                            segs, vblocks, tcol0 = prompt_window(c, 2, KA, KhA, kh, 128, vfn)
                            sb_i = c % 2
                            attend(QBD.ap[:, c, :], QBD.rows(c), 128, 64, segs, None, [], sink3[:, l, qc:qc + 1], [R_sink],
                                   S_sb[sb_i].ap, S_sb[sb_i].rs(), P_sb.ap, P_sb.rs(), tcol0, PT_sb[sb_i].ap, PT_sb[sb_i].rs(), vblocks,
                                   mixB.ap[:, mchunk, c * 64:(c + 1) * 64], mixB.rs(mchunk * NT + c * 64, mchunk * NT + (c + 1) * 64))
                        segs = [(KsA.ap[:, kh, 0:144], KsA.rows(kh), False, 144)]
                        vblocks = [(VsA.ap[:, 0, kh * 128:(kh + 1) * 128], VsA.rows(0), 128),
                                   (VsA.ap[0:TS, 1, kh * 128:(kh + 1) * 128], VsA.rows(1), TS)]
                        attend(QBDs[:, :], [R_QBDs], 32, TS, segs, None, [], sinks3[:, l, qc:qc + 1], [R_sinks],
                               Ss_ap, S_sb[0].rs(), Ps_ap, P_sb.rs(), 64, PTs_ap, PT_sb[0].rs(), vblocks,
                               mixB.ap[:, mchunk, T:NT], mixB.rs(mchunk * NT + T, (mchunk + 1) * NT))

            for c in range(8):
                dve(lambda e, c=c: e.scalar_tensor_tensor(out=mixB.ap[:, 4 + c, 0:T], in0=PgB.ap[:, c, :], scalar=hin[:, c:c + 1],
                                                          in1=mixB.ap[:, 4 + c, 0:T], op0=ALU.mult, op1=ALU.add),
                    PgB.rows(c) + [R_hin] + mixB.rs((4 + c) * NT, (4 + c) * NT + T), mixB.rs((4 + c) * NT, (4 + c) * NT + T))
            dve(lambda e: e.tensor_tensor(out=hlt[:, :], in0=Pl[:, :], in1=hin[:, :], op=ALU.mult), [R_Pl, R_hin], [R_hlt])
            dve(lambda e, l=l: e.tensor_tensor(out=hl_s[:, l * 8:(l + 1) * 8], in0=hlt[:, :], in1=h0l[:, :], op=ALU.add), [R_hlt, R_h0l], [R_hl])

            for b in range(8):
                sl = w_next()
                w16 = sl.ap.rearrange("p (c n) -> p c n", c=16)
                for mm in range(2):
                    m = b * 2 + mm
                    bset = (0, 1, 2) if m % 2 == 0 else (3, 4, 5)
                    gemm_fm(w16, mm * 128, mixB, 16, bset, sl.rs())
                    act(lambda e, m=m, bset=bset: e.activation(out=bigB.ap[:, m, 0:512], in_=banks[bset[0]][:, :], func=AF.Identity),
                        [], [R_bank[bset[0]]] + bigB.rs(m * NT, m * NT + 512))
                    dve(lambda e, m=m, bset=bset: e.tensor_copy(out=bigB.ap[:, m, 512:T], in_=banks[bset[1]][:, :]),
                        [], [R_bank[bset[1]]] + bigB.rs(m * NT + 512, m * NT + T))
                    dve(lambda e, m=m, bset=bset: e.tensor_copy(out=bigB.ap[:, m, T:NT], in_=banks[bset[2]][:, 0:TS]),
                        [], [R_bank[bset[2]]] + bigB.rs(m * NT + T, (m + 1) * NT))
            reload_x()
            post_norm_add(l, 0)
            norm_mod(l, 1)
            spill_x()
            rflip = 0
            for hb in range(8):
                for b in range(4):
                    sl = w_next()
                    w16 = sl.ap.rearrange("p (c n) -> p c n", c=16)
                    for mm in range(2):
                        j = b * 2 + mm
                        bset = (0, 1, 2) if j % 2 == 0 else (3, 4, 5)
                        gemm_fm(w16, mm * 128, hB, 16, bset, sl.rs())
                        for ti, (t0, tn) in enumerate(TILES):
                            bk = bset[ti]
                            rs_ = relu_s[rflip % 2]
                            rflip += 1
                            act(lambda e, bk=bk, tn=tn, rs_=rs_: e.activation(out=rs_.ap[:, 0:tn], in_=banks[bk][:, 0:tn], func=AF.Relu),
                                [], [R_bank[bk]] + rs_.rs())
                            dve(lambda e, j=j, t0=t0, tn=tn, rs_=rs_: e.tensor_tensor(out=uB.ap[:, j, t0:t0 + tn], in0=rs_.ap[:, 0:tn], in1=rs_.ap[:, 0:tn], op=ALU.mult),
                                rs_.rs(), uB.rs(j * NT + t0, j * NT + t0 + tn))
                for mg in range(4):
                    sl = w_next()
                    w8 = sl.ap.rearrange("p (c n) -> p c n", c=8)
                    for mm in range(4):
                        m = mg * 4 + mm
                        bset = (0, 1, 2) if m % 2 == 0 else (3, 4, 5)
                        gemm_fm(w8, mm * 128, uB, 8, bset, sl.rs())
                        for ti, (t0, tn) in enumerate(TILES):
                            bk = bset[ti]
                            dst = bigB.ap[:, m, t0:t0 + tn]
                            dres = bigB.rs(m * NT + t0, m * NT + t0 + tn)
                            if hb == 0:
                                if ti == 0:
                                    act(lambda e, bk=bk, dst=dst, tn=tn: e.activation(out=dst, in_=banks[bk][:, 0:tn], func=AF.Identity), [], [R_bank[bk]] + dres)
                                else:
                                    dve(lambda e, bk=bk, dst=dst, tn=tn: e.tensor_copy(out=dst, in_=banks[bk][:, 0:tn]), [], [R_bank[bk]] + dres)
                            else:
                                dve(lambda e, bk=bk, dst=dst, tn=tn: e.tensor_tensor(out=dst, in0=banks[bk][:, 0:tn], in1=dst, op=ALU.add), [], [R_bank[bk]] + dres)
            reload_x()
            post_norm_add(l, 1)
            if l + 1 < nl:
                norm_mod(l + 1, 0)
        sp_dma(yT_o, xB.ap.rearrange("p c t -> p (c t)"), xB.rs(), [R_out])
        sp_dma(hl_o, hl_s[:, :], [R_hl], [R_out])
        sp_dma(cl_o, cl_s[:, :], [R_cl], [R_out])
        sp_dma(hls_o, hls_s[:, :], [R_hls], [R_out])
        sp_dma(cls_o, cls_s[:, :], [R_cls], [R_out])
        k.wait_all("sp", [R_out, R_xb1, R_xb2, R_xs, R_yb1, R_yb2])
        k.emit()
        build_program.stats = {n: len(E.ops) for n, E in k.eng.items()}
        build_program.stats["nsem"] = k.nsem
    return nc


_OFF = dict(qa=0, ka=512, va=640, xb=768, gb=1792, qc=2816, kc=3328, vc=3840)


def _pp(v, n):
    v = np.asarray(v, np.float32)
    lead = v.shape[:-1]
    v = v.reshape(lead + (n, 128))
    v = np.moveaxis(v, -1, 0)
    return np.ascontiguousarray(v)


def _shared_inputs(inp, nl, ncores):
    f = np.float32
    w_in = np.asarray(inp["w_in"], f)[:nl]
    cols = []
    xb0, gb0 = _OFF["xb"], _OFF["gb"]
    for i in range(4):
        cols.append(w_in[:, :, xb0 + 256 * i:xb0 + 256 * (i + 1)])
    cols.append(w_in[:, :, _OFF["kc"]:_OFF["kc"] + 512])
    ka = w_in[:, :, _OFF["ka"]:_OFF["ka"] + 128]
    cols += [ka[:, :, 0:64], ka[:, :, 0:64], ka[:, :, 64:128], ka[:, :, 64:128]]
    cols.append(w_in[:, :, _OFF["vc"]:_OFF["vc"] + 512])
    va = w_in[:, :, _OFF["va"]:_OFF["va"] + 128]
    cols += [va[:, :, 0:64], va[:, :, 0:64], va[:, :, 64:128], va[:, :, 64:128]]
    for c in range(8):
        cols.append(w_in[:, :, xb0 + 128 * c:xb0 + 128 * (c + 1)])
        cols.append(w_in[:, :, gb0 + 128 * c:gb0 + 128 * (c + 1)])
    cols.append(w_in[:, :, _OFF["qa"]:_OFF["qa"] + 512])
    cols.append(w_in[:, :, _OFF["qc"]:_OFF["qc"] + 512])
    win = np.ascontiguousarray(np.concatenate(cols, axis=2))
    assert win.shape[2] == WCOLS, win.shape
    sh = {}
    sh["win"] = win
    sh["wout"] = np.ascontiguousarray(np.asarray(inp["w_out"], f)[:nl])
    sh["wup"] = np.ascontiguousarray(np.asarray(inp["w_up"], f)[:nl])
    sh["wdn"] = np.ascontiguousarray(np.asarray(inp["w_down"], f)[:nl])
    sh["bmod"] = _pp(np.asarray(inp["b_mod"], f)[:nl], 96).reshape(128, nl * 96)
    gv = np.stack([np.asarray(inp[n], f)[:nl] for n in ("g_pre_mix", "g_post_mix", "g_pre_mlp", "g_post_mlp")], axis=1)
    sh["gv"] = _pp(gv, 16).reshape(128, nl * 4 * 16)
    cw = np.asarray(inp["lru_conv_w"], f)[:nl]
    parts = [cw[:, j] for j in range(4)] + [np.asarray(inp["lru_conv_b"], f)[:nl],
                                             np.asarray(inp["lru_b_r"], f)[:nl].reshape(nl, 1024),
                                             np.asarray(inp["lru_b_i"], f)[:nl].reshape(nl, 1024),
                                             np.asarray(inp["lru_lambda"], f)[:nl]]
    lr = np.stack(parts, axis=1)
    lr = _pp(lr, 8)
    sh["lrup"] = np.ascontiguousarray(lr.transpose(0, 1, 3, 2)).reshape(128, nl * 64)
    wri = np.zeros((nl, 8, 2, 128, 128), f)
    for gi, name in enumerate(("lru_w_r", "lru_w_i")):
        w = np.asarray(inp[name], f)[:nl]
        for c in range(8):
            wri[:, c, gi, 0:64, 0:64] = w[:, 2 * c]
            wri[:, c, gi, 64:128, 64:128] = w[:, 2 * c + 1]
    sh["wri"] = wri.reshape(nl * 16 * 128, 128)
    tab = np.asarray(inp["band_rel_bias"], f)[:nl]
    i = np.arange(64)[:, None]
    j = np.arange(576)[None, :]
    idx = np.clip(i + 512 - j, -128, 128) + 128
    bp = tab[:, :, idx]
    sh["biasp"] = np.ascontiguousarray(bp.reshape(nl, 4, 128, 576)).reshape(nl * 4 * 128, 576)
    i = np.arange(TS)[:, None]
    j = np.arange(528)[None, :]
    idx = np.clip(i + 512 - j, -128, 128) + 128
    bs = tab[:, :, idx]
    sh["biass"] = np.ascontiguousarray(bs.reshape(nl, 4, 32, 528)).reshape(nl * 4 * 32, 528)
    sk = np.asarray(inp["swa_sink"], f)[:nl]
    sink = np.zeros((128, nl, 4), f)
    sinks = np.zeros((32, nl, 4), f)
    for qc in range(4):
        for s in range(2):
            sink[64 * s:64 * s + 64, :, qc] = sk[None, :, 2 * qc + s]
            sinks[16 * s:16 * s + 16, :, qc] = sk[None, :, 2 * qc + s]
    sh["sink"] = sink.reshape(128, nl * 4)
    sh["sinks"] = sinks.reshape(32, nl * 4)
    sh["ident"] = np.eye(128, dtype=f)
    return sh


def _core_inputs(inp, sh, core, nl, ncores, cores_global):
    f = np.float32
    g = cores_global[core]
    b, half = g // 2, g % 2
    d = dict(sh)
    xp = np.asarray(inp["x_prompt"], f)[b, half * T:(half + 1) * T]
    xs = np.asarray(inp["x_sample"], f)[g]
    xT = np.concatenate([xp, xs], axis=0).T
    d["xT"] = np.ascontiguousarray(xT.reshape(16, 128, NT).transpose(1, 0, 2)).reshape(128, 16 * NT)
    call = np.concatenate([np.asarray(inp["c_prompt"], f), np.asarray(inp["c_sample"], f)], axis=0)
    d["cT"] = np.ascontiguousarray(call.T.reshape(16, 128, 12).transpose(1, 0, 2)).reshape(128, 16 * 12)
    nsh = 12288 // ncores
    d["wmod"] = np.ascontiguousarray(np.asarray(inp["w_mod"], f)[:nl, :, core * nsh:(core + 1) * nsh])
    sel = np.zeros((128, 24), f)
    sel[:, b] = 1.0
    sel[:, 12 + 4 + g] = 1.0
    d["sel"] = sel
    flags = np.zeros((128, 2), f)
    flags[:, 0] = -1e30 if half == 0 else 0.0
    flags[:, 1] = 0.0 if half == 0 else 1.0
    d["flags"] = flags
    ck = np.asarray(inp["cache_swa_k"], f)[:nl, g]
    ckT = ck.transpose(0, 2, 3, 1)
    d["cswak"] = np.ascontiguousarray(np.concatenate([ckT, ckT], axis=2)).reshape(nl * 2 * 128, 128)
    cv = np.asarray(inp["cache_swa_v"], f)[:nl, g]
    d["cswav"] = np.ascontiguousarray(np.concatenate([cv[:, :, 0], cv[:, :, 0], cv[:, :, 1], cv[:, :, 1]], axis=2)).reshape(nl * 128, 256)
    bk = np.asarray(inp["cache_band_k"], f)[:nl, g].reshape(nl, 512, 512)
    d["cbk"] = np.ascontiguousarray(bk.transpose(0, 2, 1)).reshape(nl * 512, 512)
    d["cbv"] = np.ascontiguousarray(np.asarray(inp["cache_band_v"], f)[:nl, g].reshape(nl * 512, 512))
    d["sth"] = _pp(np.asarray(inp["state_lru_h"], f)[:nl, g], 8).reshape(128, nl * 8)
    stc = np.asarray(inp["state_lru_conv"], f)[:nl, g]
    stc = _pp(stc, 8)
    d["stc"] = np.ascontiguousarray(stc.transpose(0, 1, 3, 2)).reshape(128, nl * 24)
    return d


_PROG_CACHE = {}


def _run(inp, nl=NLAYER, cores_global=None):
    if cores_global is None:
        cores_global = list(range(8))
    ncores = len(cores_global)
    key = (nl, ncores)
    if key not in _PROG_CACHE:
        _PROG_CACHE[key] = build_program(nl, ncores)
    nc = _PROG_CACHE[key]
    sh = _shared_inputs(inp, nl, ncores)
    in_maps = [_core_inputs(inp, sh, i, nl, ncores, cores_global) for i in range(ncores)]
    res = run_bass_kernel_spmd(nc, in_maps, core_ids=list(range(ncores)))
    return res.results


def _unpp(a, lead, n):
    a = np.moveaxis(a, 0, -1)
    return np.ascontiguousarray(a).reshape(tuple(lead) + (n * 128,))


def _assemble(results, nl, cores_global, nbatch, ndec):
    f = np.float32
    y_p = np.zeros((nbatch, 2 * T, D), f)
    y_s = np.zeros((ndec, TS, D), f)
    swa_kp = np.zeros((nl, nbatch, 128, 2, 64), f)
    swa_vp = np.zeros((nl, nbatch, 128, 2, 64), f)
    band_kp = np.zeros((nl, nbatch, 512, 8, 64), f)
    band_vp = np.zeros((nl, nbatch, 512, 8, 64), f)
    lru_hp = np.zeros((nl, nbatch, 1024), f)
    lru_cp = np.zeros((nl, nbatch, 3, 1024), f)
    swa_ks = np.zeros((nl, ndec, TS, 2, 64), f)
    swa_vs = np.zeros((nl, ndec, TS, 2, 64), f)
    band_ks = np.zeros((nl, ndec, TS, 8, 64), f)
    band_vs = np.zeros((nl, ndec, TS, 8, 64), f)
    lru_hs = np.zeros((nl, ndec, 1024), f)
    lru_cs = np.zeros((nl, ndec, 3, 1024), f)
    for ci, g in enumerate(cores_global):
        r = results[ci]
        b, half = g // 2, g % 2
        yT = r["yT"].reshape(128, 16, NT).transpose(1, 0, 2).reshape(D, NT)
        y_p[b, half * T:(half + 1) * T] = yT[:, :T].T
        y_s[g] = yT[:, T:].T
        if half == 1:
            kc = r["kc_o"].reshape(nl, 512, 512)
            band_kp[:, b] = kc.transpose(0, 2, 1).reshape(nl, 512, 8, 64)
            band_vp[:, b] = r["vc_o"].reshape(nl, 512, 8, 64)
            ka = r["ka_o"].reshape(nl, 2, 128, 128)[:, :, 0:64, :]
            swa_kp[:, b] = ka.transpose(0, 3, 1, 2)
            va = r["va_o"].reshape(nl, 128, 4, 64)[:, :, [0, 2], :]
            swa_vp[:, b] = va
            lru_hp[:, b] = _unpp(r["hl_o"].reshape(128, nl, 8), (nl,), 8)
            cl = r["cl_o"].reshape(128, nl, 8, 3).transpose(0, 1, 3, 2)
            lru_cp[:, b] = _unpp(cl, (nl, 3), 8)
        kcs = r["kcs_o"].reshape(nl, 512, TS)
        band_ks[:, g] = kcs.transpose(0, 2, 1).reshape(nl, TS, 8, 64)
        band_vs[:, g] = r["vcs_o"].reshape(nl, TS, 8, 64)
        kas = r["kas_o"].reshape(nl, 2, 128, TS)[:, :, 0:64, :]
        swa_ks[:, g] = kas.transpose(0, 3, 1, 2)
        swa_vs[:, g] = r["vas_o"].reshape(nl, TS, 4, 64)[:, :, [0, 2], :]
        lru_hs[:, g] = _unpp(r["hls_o"].reshape(128, nl, 8), (nl,), 8)
        cls = r["cls_o"].reshape(128, nl, 8, 3).transpose(0, 1, 3, 2)
        lru_cs[:, g] = _unpp(cls, (nl, 3), 8)
    return (y_p, y_s, swa_kp, swa_vp, band_kp, band_vp, lru_hp, lru_cp,
            swa_ks, swa_vs, band_ks, band_vs, lru_hs, lru_cs)


def kernel(**inputs):
    results = _run(inputs, NLAYER, list(range(8)))
    return _assemble(results, NLAYER, list(range(8)), 4, 8)
```

```python
import contextlib
import numpy as np
import concourse.bass as bass
import concourse.mybir as mybir
from concourse.bass_utils import run_bass_kernel_spmd

F32 = mybir.dt.float32
BF16 = mybir.dt.bfloat16
AF = mybir.ActivationFunctionType
ALU = mybir.AluOpType
AX = mybir.AxisListType

NLAYER = 4
D = 2048
NCH = 16
T = 1024
TS = 16
NT = T + TS
WCOLS = 22 * 256
G = 256


class Res:
    __slots__ = ("name", "w", "r", "dsem", "dcount")

    def __init__(self, name):
        self.name = name
        self.w = {}
        self.r = {}
        self.dsem = None
        self.dcount = 0


class Eng:
    def __init__(self, name, sem):
        self.name = name
        self.sem = sem
        self.count = 0
        self.ops = []
        self.waited = {}


class K:
    ENGS = ["pe", "act", "dve", "pool", "sp"]

    def __init__(self, nc, stack):
        self.nc = nc
        self.stack = stack
        self.eng = {}
        for n in self.ENGS:
            self.eng[n] = Eng(n, stack.enter_context(nc.semaphore("s_" + n)))
        self.nsem = len(self.ENGS)
        self.semid = {}

    def sb(self, name, shape, dtype):
        return self.stack.enter_context(self.nc.sbuf_tensor(name, shape, dtype))

    def ps(self, name, shape, dtype=F32):
        return self.stack.enter_context(self.nc.psum_tensor(name, shape, dtype))

    def newsem(self, name):
        self.nsem += 1
        return self.stack.enter_context(self.nc.semaphore(name))

    def _collect(self, E, reads, writes, selfwait):
        waits = {}
        for R in reads:
            for sem, val in R.w.items():
                if waits.get(sem, 0) < val:
                    waits[sem] = val
        for R in writes:
            for sem, val in R.w.items():
                if waits.get(sem, 0) < val:
                    waits[sem] = val
            for sem, val in R.r.items():
                if waits.get(sem, 0) < val:
                    waits[sem] = val
        wl = []
        for sem, val in waits.items():
            if (sem is E.sem) and not selfwait:
                continue
            if E.waited.get(sem, 0) >= val:
                continue
            E.waited[sem] = val
            wl.append((sem, val))
        return wl

    @staticmethod
    def _mark(ev, reads, writes):
        s, v = ev
        for R in reads:
            if R.r.get(s, 0) < v:
                R.r[s] = v
        for R in writes:
            if R.w.get(s, 0) < v:
                R.w[s] = v

    def op(self, eng, fn, reads=(), writes=(), inc=True, selfwait=None):
        E = self.eng[eng]
        if selfwait is None:
            selfwait = eng != "pe"
        wl = self._collect(E, reads, writes, selfwait)
        if inc:
            E.count += 1
            ev = (E.sem, E.count)
        else:
            ev = (E.sem, E.count + 1)
        self._mark(ev, reads, writes)
        E.ops.append((wl, fn, (E.sem, 1) if inc else None))

    def dma(self, eng, fn, reads=(), writes=()):
        E = self.eng[eng]
        wl = self._collect(E, reads, writes, True)
        R0 = writes[0] if len(writes) else reads[0]
        if R0.dsem is None:
            R0.dsem = self.newsem("d_" + R0.name)
        R0.dcount += 16
        ev = (R0.dsem, R0.dcount)
        self._mark(ev, reads, writes)
        E.ops.append((wl, fn, (R0.dsem, 16)))

    def custom(self, eng, fn, sem, reads=(), writes=()):
        E = self.eng[eng]
        wl = self._collect(E, reads, writes, True)
        cnt = self.semid.get(id(sem), 0) + 1
        self.semid[id(sem)] = cnt
        self._mark((sem, cnt), reads, writes)
        E.ops.append((wl, fn, (sem, 1)))

    def wait_all(self, eng, ress):
        E = self.eng[eng]
        wl = self._collect(E, [], ress, True)
        E.ops.append((wl, None, None))

    def emit(self):
        nc = self.nc
        with nc.Block() as block:
            def replay(E, engine):
                for wl, fn, inc in E.ops:
                    for sem, val in wl:
                        engine.wait_ge(sem, val)
                    if fn is None:
                        continue
                    ins = fn(engine)
                    if inc is not None:
                        ins.then_inc(inc[0], inc[1])

            @block.tensor
            def _(e):
                replay(self.eng["pe"], e)

            @block.scalar
            def _(e):
                replay(self.eng["act"], e)

            @block.vector
            def _(e):
                replay(self.eng["dve"], e)

            @block.gpsimd
            def _(e):
                replay(self.eng["pool"], e)

            @block.sync
            def _(e):
                replay(self.eng["sp"], e)


class Arena:
    def __init__(self, k, name, nbytes):
        self.nbytes = nbytes
        self.t = k.sb(name, [128, nbytes // 4], F32)
        self.gr = [Res(f"{name}_g{i}") for i in range((nbytes + G - 1) // G)]


class Buf:
    def __init__(self, arena, off, dtype, shape):
        self.arena = arena
        self.off = off
        self.dtype = dtype
        self.shape = tuple(shape)
        self.esz = 2 if dtype == BF16 else 4
        n = int(np.prod(shape))
        self.nbytes = n * self.esz
        assert off % 4 == 0 and self.nbytes % 4 == 0, (off, self.nbytes)
        assert off + self.nbytes <= arena.nbytes, (off, self.nbytes, arena.nbytes)
        v = arena.t[:, off // 4:(off + self.nbytes) // 4]
        if dtype != F32:
            v = v.bitcast(dtype)
        if len(shape) == 2:
            v = v.rearrange("p (a b) -> p a b", a=shape[0])
        elif len(shape) == 3:
            v = v.rearrange("p (a b c) -> p a b c", a=shape[0], b=shape[1])
        self.ap = v

    def rs(self, lo=None, hi=None):
        if lo is None:
            lo, hi = 0, self.nbytes // self.esz
        b0 = self.off + lo * self.esz
        b1 = self.off + hi * self.esz
        return self.arena.gr[b0 // G:(b1 - 1) // G + 1]

    def rows(self, i, j=None):
        per = int(np.prod(self.shape[1:]))
        j = i + 1 if j is None else j
        return self.rs(i * per, j * per)


def build_program(nl, ncores):
    nc = bass.Bass("TRN2", target_bir_lowering=False)
    NSH = 12288 // ncores
    MCH = NSH // 128
    MBLK = NSH // 256

    def din(name, shape):
        return nc.dram_tensor(name, list(shape), F32, kind="ExternalInput").ap()

    def dout(name, shape):
        return nc.dram_tensor(name, list(shape), F32, kind="ExternalOutput").ap()

    xT_d = din("xT", [128, NCH * NT])
    cT_d = din("cT", [128, NCH * 12])
    wmod_d = din("wmod", [nl, D, NSH])
    bmod_d = din("bmod", [128, nl * 96])
    sel_d = din("sel", [128, 24])
    gv_d = din("gv", [128, nl * 4 * NCH])
    win_d = din("win", [nl, D, WCOLS])
    wout_d = din("wout", [nl, D, D])
    wup_d = din("wup", [nl, D, 4 * D])
    wdn_d = din("wdn", [nl, 4 * D, D])
    lrup_d = din("lrup", [128, nl * 8 * 8])
    wri_d = din("wri", [nl * 16 * 128, 128])
    biasp_d = din("biasp", [nl * 4 * 128, 576])
    biass_d = din("biass", [nl * 4 * 32, 528])
    sink_d = din("sink", [128, nl * 4])
    sinks_d = din("sinks", [32, nl * 4])
    flags_d = din("flags", [128, 2])
    ident_d = din("ident", [128, 128])
    cswak_d = din("cswak", [nl * 2 * 128, 128])
    cswav_d = din("cswav", [nl * 128, 256])
    cbk_d = din("cbk", [nl * 4 * 128, 512])
    cbv_d = din("cbv", [nl * 512, 512])
    sth_d = din("sth", [128, nl * 8])
    stc_d = din("stc", [128, nl * 8 * 3])

    yT_o = dout("yT", [128, NCH * NT])
    kc_o = dout("kc_o", [nl * 512, 512])
    vc_o = dout("vc_o", [nl * 512, 512])
    ka_o = dout("ka_o", [nl * 2 * 128, 128])
    va_o = dout("va_o", [nl * 128, 256])
    hl_o = dout("hl_o", [128, nl * 8])
    cl_o = dout("cl_o", [128, nl * 8 * 3])
    kcs_o = dout("kcs_o", [nl * 512, TS])
    vcs_o = dout("vcs_o", [nl * TS, 512])
    kas_o = dout("kas_o", [nl * 2 * 128, TS])
    vas_o = dout("vas_o", [nl * TS, 256])
    hls_o = dout("hls_o", [128, nl * 8])
    cls_o = dout("cls_o", [128, nl * 8 * 3])

    xs_t = nc.dram_tensor("xs_spill", [128, NCH * NT], F32)
    X1R = 1536
    xb1_t = nc.dram_tensor("xb1", [X1R, 512], BF16)
    yb1_t = nc.dram_tensor("yb1", [2 * X1R, 512], BF16)
    xb2_t = nc.dram_tensor("xb2", [128, 8], F32)
    yb2_t = nc.dram_tensor("yb2", [256, 8], F32)
    xb0_t = nc.dram_tensor("xb0", [128, nl * MCH * 12], F32)
    yb0_t = nc.dram_tensor("yb0", [ncores * 128, nl * MCH * 12], F32)
    pair_groups = [[2 * i, 2 * i + 1] for i in range(ncores // 2)]
    all_group = [list(range(ncores))]

    with contextlib.ExitStack() as st:
        k = K(nc, st)
        SZ_H = NCH * NT * 2
        SZ_X = NCH * NT * 4
        OFF_H = 0
        OFF_BIG = OFF_H + SZ_H
        OFF_X = OFF_BIG + SZ_X
        OFF_SLOT = OFF_X + SZ_X
        NSLOT = 3
        A = Arena(k, "arena", OFF_SLOT + NSLOT * 8192)
        hB = Buf(A, OFF_H, BF16, [NCH, NT])
        bigB = Buf(A, OFF_BIG, F32, [NCH, NT])
        xB = Buf(A, OFF_X, F32, [NCH, NT])
        slots = [Buf(A, OFF_SLOT + i * 8192, BF16, [4096]) for i in range(NSLOT)]
        o = OFF_BIG
        KC = Buf(A, o, BF16, [4, T]); o += 8192
        KhC = Buf(A, o, BF16, [4, 512]); o += 4096
        VC = Buf(A, o, BF16, [8, 512]); o += 8192
        VhC = Buf(A, o, BF16, [4, 512]); o += 4096
        KA = Buf(A, o, BF16, [2, T]); o += 4096
        KhA = Buf(A, o, BF16, [2, 128]); o += 512
        VA = Buf(A, o, BF16, [8, 256]); o += 4096
        VhA = Buf(A, o, BF16, [1, 256]); o += 512
        KsC = Buf(A, o, BF16, [4, 528]); o += 4352
        VsC = Buf(A, o, BF16, [5, 512]); o += 5120
        KsA = Buf(A, o, BF16, [2, 144]); o += 768
        VsA = Buf(A, o, BF16, [2, 256]); o += 1024
        stg = [Buf(A, o + i * 2048, F32, [512]) for i in range(2)]; o += 4096
        stg_s = Buf(A, o, F32, [256]); o += 1024
        xbuf = Buf(A, o, F32, [1032]); o += 4224
        lr_xc = Buf(A, o, F32, [512]); o += 2048
        lr_r = Buf(A, o, F32, [512]); o += 2048
        lr_i = Buf(A, o, F32, [512]); o += 2048
        lr_a = Buf(A, o, F32, [512]); o += 2048
        lr_h = Buf(A, o, F32, [512]); o += 2048
        assert o <= OFF_BIG + SZ_X, o - OFF_BIG
        o = OFF_X
        mixB = Buf(A, o, BF16, [NCH, NT]); o += SZ_H
        PgB = Buf(A, o, BF16, [8, T]); o += 16384
        QBD = Buf(A, o, BF16, [16, 128]); o += 4096
        biaspB = Buf(A, o, BF16, [576]); o += 2304
        S_sb = [Buf(A, o + i * 2304, F32, [576]) for i in range(2)]; o += 4608
        P_sb = [Buf(A, o + i * 1280, BF16, [640]) for i in range(2)]; o += 2560
        PT_sb = [Buf(A, o + i * 1280, BF16, [5, 128]) for i in range(2)]; o += 2560
        assert o <= OFF_X + SZ_X, o - OFF_X
        lr_p = Buf(A, S_sb[0].off, F32, [512])
        lr_g = Buf(A, S_sb[1].off, BF16, [NT])
        lr_xcb = Buf(A, PT_sb[0].off, BF16, [512])
        lr2_xcb = Buf(A, PT_sb[1].off, BF16, [512])
        lr2_xc = Buf(A, QBD.off, F32, [512])
        lr2_r = Buf(A, QBD.off + 2048, F32, [512])
        lr2_i = Buf(A, biaspB.off, F32, [512])
        lr2_a = Buf(A, P_sb[0].off, F32, [512])
        uB = Buf(A, OFF_X, BF16, [8, NT])
        relu_s = [Buf(A, OFF_X + 16640 + i * 2048, F32, [512]) for i in range(2)]
        o = OFF_BIG
        modallB = Buf(A, o, F32, [nl * 96, 12]); o += nl * 96 * 12 * 4
        mstageB = Buf(A, o, F32, [nl * MCH * 12]); o += nl * MCH * 12 * 4
        modvB = Buf(A, o, F32, [nl * 96, 2]); o += nl * 96 * 2 * 4
        bmodB = Buf(A, o, F32, [nl * 96]); o += nl * 96 * 4
        gvB = Buf(A, o, F32, [nl * 4 * NCH]); o += nl * 4 * NCH * 4
        cTB = Buf(A, o, F32, [NCH * 12]); o += NCH * 12 * 4
        cTbB = Buf(A, o, BF16, [NCH * 12]); o += NCH * 12 * 2
        selB = Buf(A, o, F32, [24]); o += 96
        assert o <= OFF_BIG + SZ_X

        def pt(name, shape, dtype=F32):
            return k.sb(name, shape, dtype), Res(name)

        shv, R_shv = pt("shv", [128, nl * 2 * NCH * 2])
        gsv, R_gsv = pt("gsv", [128, nl * 2 * NCH * 2])
        ggv, R_ggv = pt("ggv", [128, nl * 2 * NCH * 2])
        lrup_s, R_lrup = pt("lrup_s", [128, nl * 64])
        nls_s, R_nls = pt("nls_s", [128, nl * 8 * 2])
        wri_s, R_wri = pt("wri_s", [128, 16 * 128], BF16)
        sink_s, R_sink = pt("sink_s", [128, nl * 4])
        sinks_s, R_sinks = pt("sinks_s", [32, nl * 4])
        flags_s, R_flags = pt("flags_s", [128, 2])
        ident_b, R_ident = pt("ident_b", [128, 128], BF16)
        ones_b, R_ones = pt("ones_b", [128, 128], BF16)
        cst, R_cst = pt("cst", [128, 4])
        rstd, R_rstd = pt("rstd", [128, NT])
        biass_s, R_biass = pt("biass_s", [32, 528], BF16)
        maskrow, R_maskrow = pt("maskrow", [1, 512], BF16)
        sinkb, R_sinkb = pt("sinkb", [128, nl * 4], BF16)
        sinksb, R_sinksb = pt("sinksb", [32, nl * 4], BF16)
        sth_s, R_sth = pt("sth_s", [128, nl * 8])
        stc_s, R_stc = pt("stc_s", [128, nl * 24])
        xbt, R_xbt = pt("xbt", [128, 24])
        pref, R_pref = pt("pref", [128, 24])
        hin, R_hin = pt("hin", [128, 8])
        h0l, R_h0l = pt("h0l", [128, 8])
        Pl, R_Pl = pt("Pl", [128, 8])
        hlt, R_hlt = pt("hlt", [128, 8])
        sm, _ = pt("sm", [128, 8])
        R_smp = [Res("sm0"), Res("sm1")]
        car, R_car = pt("car", [128, 2])
        sx, R_sx = pt("sx", [128, 20])
        sxc, R_sxc = pt("sxc", [128, TS])
        sxcb, R_sxcb = pt("sxcb", [128, TS], BF16)
        sr, R_sr = pt("sr", [128, TS])
        si, R_si = pt("si", [128, TS])
        sa, R_sa = pt("sa", [128, TS])
        sh, R_sh = pt("sh", [128, TS])
        hls_s, R_hls = pt("hls_s", [128, nl * 8])
        cls_s, R_cls = pt("cls_s", [128, nl * 24])
        hl_s, R_hl = pt("hl_s", [128, nl * 8])
        cl_s, R_cl = pt("cl_s", [128, nl * 24])
        QBDs, R_QBDs = pt("QBDs", [128, 32], BF16)

        PP = [k.ps(f"pp{i}", [128, 1024]) for i in range(4)]
        banks = [PP[i // 2][:, (i % 2) * 512:(i % 2 + 1) * 512] for i in range(8)]
        R_bank = [Res(f"bank{i}") for i in range(8)]
        Sps = [PP[2], PP[3]]
        R_SpsL = [[R_bank[4], R_bank[5]], [R_bank[6], R_bank[7]]]
        S2 = PP[3]
        R_S2L = R_SpsL[1]
        PTb16 = PP[1][:, 512:1024].bitcast(BF16)

        R_xs = Res("xs_dram")
        R_xb1 = Res("xb1")
        R_yb1 = Res("yb1")
        R_xb2 = Res("xb2")
        R_yb2 = Res("yb2")
        R_xb0 = Res("xb0")
        R_yb0 = Res("yb0")
        R_out = Res("outs")
        cc_sem = k.newsem("cc")

        wq = []
        wstate = {"issued": 0, "used": 0}

        def w_issue_upto(n):
            while wstate["issued"] < min(n, len(wq)):
                i = wstate["issued"]
                sl = slots[i % NSLOT]
                src, shp = wq[i]
                dst = sl.ap.rearrange("p (c n) -> p c n", c=(16 if shp == "k16" else 8))
                k.dma("pool", lambda e, dst=dst, src=src: e.dma_start(out=dst, in_=src), reads=[], writes=sl.rs())
                wstate["issued"] += 1

        def w_next():
            i = wstate["used"]
            w_issue_upto(i + NSLOT)
            wstate["used"] += 1
            return slots[i % NSLOT]

        def wsrc_k16(dram, l, c0):
            return (dram[l].rearrange("(c p) n -> p c n", p=128)[:, :, c0:c0 + 256], "k16")

        def wsrc_dn(l, hb, mg):
            return (wdn_d[l, hb * 1024:(hb + 1) * 1024, mg * 512:(mg + 1) * 512].rearrange("(c p) n -> p c n", p=128), "k8")

        for l in range(nl):
            for b in range(MBLK):
                wq.append(wsrc_k16(wmod_d, l, b * 256))
        for l in range(nl):
            for b in range(22):
                wq.append(wsrc_k16(win_d, l, b * 256))
            for b in range(8):
                wq.append(wsrc_k16(wout_d, l, b * 256))
            for hb in range(8):
                for b in range(4):
                    wq.append(wsrc_k16(wup_d, l, hb * 1024 + b * 256))
                for mg in range(4):
                    wq.append(wsrc_dn(l, hb, mg))

        def act(fn, reads, writes):
            k.op("act", fn, reads, writes)

        def dve(fn, reads, writes):
            k.op("dve", fn, reads, writes)

        def pe(fn, reads, writes, inc=True):
            k.op("pe", fn, reads, writes, inc=inc)

        def sp_dma(out, in_, reads, writes):
            k.dma("sp", lambda e: e.dma_start(out=out, in_=in_), reads=reads, writes=writes)

        def pool_dma(out, in_, reads, writes):
            k.dma("pool", lambda e: e.dma_start(out=out, in_=in_), reads=reads, writes=writes)

        TILES = [(0, 512), (512, 512), (T, TS)]

        def gemm_fm(w3, col0, rhsB, nk, bset, wres):
            for kc in range(nk):
                lw = w3[:, kc, col0:col0 + 128]
                last = (kc == nk - 1)
                for ti, (t0, tn) in enumerate(TILES):
                    bk = bset[ti]
                    pe(lambda e, bk=bk, lw=lw, kc=kc, t0=t0, tn=tn, last=last:
                       e.matmul(banks[bk][:, 0:tn], lw, rhsB.ap[:, kc, t0:t0 + tn], start=(kc == 0), stop=last),
                       reads=wres + rhsB.rows(kc), writes=[R_bank[bk]], inc=(last and ti == 2))

        sp_dma(cTB.ap, cT_d, [], cTB.rs())
        sp_dma(bmodB.ap, bmod_d, [], bmodB.rs())
        sp_dma(selB.ap, sel_d, [], selB.rs())
        sp_dma(gvB.ap, gv_d, [], gvB.rs())
        sp_dma(lrup_s[:, :], lrup_d, [], [R_lrup])
        sp_dma(sink_s[:, :], sink_d, [], [R_sink])
        sp_dma(sinks_s[:, :], sinks_d, [], [R_sinks])
        sp_dma(flags_s[:, :], flags_d, [], [R_flags])
        sp_dma(sth_s[:, :], sth_d, [], [R_sth])
        sp_dma(stc_s[:, :], stc_d, [], [R_stc])
        pool_dma(ident_b[:, :], ident_d, [], [R_ident])
        sp_dma(xB.ap.rearrange("p c t -> p (c t)"), xT_d, [], xB.rs())
        dve(lambda e: e.memset(ones_b[:, :], 1.0), [], [R_ones])
        dve(lambda e: e.memset(cst[:, 0:1], 1e-6), [], [R_cst])
        dve(lambda e: e.memset(cst[:, 1:2], 1.0), [], [R_cst])
        dve(lambda e: e.memset(cst[:, 2:3], 0.0), [], [R_cst])
        dve(lambda e: e.memset(QBDs[:, :], 0.0), [], [R_QBDs])
        zeros_bc = cst[:, 2:3].to_broadcast([128, 512])
        dve(lambda e: e.memset(maskrow[:, :], 0.0), [], [R_maskrow])
        dve(lambda e: e.tensor_scalar(out=maskrow[:, :], in0=maskrow[:, :], scalar1=flags_s[0:1, 0:1], scalar2=None, op0=ALU.add), [R_maskrow, R_flags], [R_maskrow])
        dve(lambda e: e.tensor_copy(out=sinkb[:, :], in_=sink_s[:, :]), [R_sink], [R_sinkb])
        dve(lambda e: e.tensor_copy(out=sinksb[:, :], in_=sinks_s[:, :]), [R_sinks], [R_sinksb])
        lam_v = lrup_s[:, :].rearrange("p (n e) -> p n e", e=8)[:, :, 7]
        nls_v = nls_s[:, :].rearrange("p (n e) -> p n e", e=2)
        act(lambda e: e.activation(out=nls_v[:, :, 0], in_=lam_v, func=AF.Exp, scale=-1.0), [R_lrup], [R_nls])
        act(lambda e: e.activation(out=nls_v[:, :, 0], in_=nls_v[:, :, 0], func=AF.Ln, bias=cst[:, 1:2], scale=1.0), [R_cst, R_nls], [R_nls])
        dve(lambda e: e.tensor_scalar(out=nls_v[:, :, 1], in0=nls_v[:, :, 0], scalar1=-16.0, scalar2=None, op0=ALU.mult), [R_nls], [R_nls])
        dve(lambda e: e.tensor_scalar(out=nls_v[:, :, 0], in0=nls_v[:, :, 0], scalar1=-8.0, scalar2=None, op0=ALU.mult), [R_nls], [R_nls])
        act(lambda e: e.activation(out=cTbB.ap, in_=cTB.ap, func=AF.Silu), cTB.rs(), cTbB.rs())

        cTv = cTbB.ap.rearrange("p (c j) -> p c j", j=12)
        for l in range(nl):
            for b in range(MBLK):
                sl = w_next()
                w16 = sl.ap.rearrange("p (c n) -> p c n", c=16)
                for mm in range(2):
                    mc = b * 2 + mm
                    bk = mc % 2
                    for kc in range(16):
                        last = kc == 15
                        pe(lambda e, bk=bk, w16=w16, kc=kc, mm=mm, last=last:
                           e.matmul(banks[bk][:, 0:12], w16[:, kc, mm * 128:(mm + 1) * 128], cTv[:, kc, :], start=(kc == 0), stop=last),
                           reads=sl.rs() + cTbB.rs(), writes=[R_bank[bk]], inc=last)
                    o0 = (l * MCH + mc) * 12
                    dve(lambda e, bk=bk, o0=o0: e.tensor_copy(out=mstageB.ap[:, o0:o0 + 12], in_=banks[bk][:, 0:12]),
                        [], [R_bank[bk]] + mstageB.rs(o0, o0 + 12))
        sp_dma(xb0_t.ap(), mstageB.ap, mstageB.rs(), [R_xb0])
        k.custom("pool", lambda e: e.collective_compute("AllGather", ALU.bypass, replica_groups=all_group,
                                                        ins=[xb0_t.ap().opt()], outs=[yb0_t.ap().opt()]),
                 cc_sem, reads=[R_xb0], writes=[R_yb0])
        mav = modallB.ap.rearrange("p (l r m) j -> p l r m j", l=nl, r=ncores)
        for r in range(ncores):
            for l in range(nl):
                src = yb0_t.ap()[r * 128:(r + 1) * 128, l * MCH * 12:(l + 1) * MCH * 12].rearrange("p (m j) -> p m j", j=12)
                k.dma("sp", lambda e, r=r, l=l, src=src: e.dma_start(out=mav[:, l, r, :, :], in_=src), reads=[R_yb0], writes=modallB.rs())
        for w in range(2):
            for j in range(12):
                if j == 0:
                    dve(lambda e, w=w: e.tensor_scalar(out=modvB.ap[:, :, w], in0=modallB.ap[:, :, 0], scalar1=selB.ap[:, w * 12:w * 12 + 1],
                                                       scalar2=None, op0=ALU.mult), modallB.rs() + selB.rs(), modvB.rs())
                else:
                    dve(lambda e, w=w, j=j: e.scalar_tensor_tensor(out=modvB.ap[:, :, w], in0=modallB.ap[:, :, j], scalar=selB.ap[:, w * 12 + j:w * 12 + j + 1],
                                                                   in1=modvB.ap[:, :, w], op0=ALU.mult, op1=ALU.add),
                        modallB.rs() + selB.rs() + modvB.rs(), modvB.rs())
            dve(lambda e, w=w: e.tensor_tensor(out=modvB.ap[:, :, w], in0=modvB.ap[:, :, w], in1=bmodB.ap, op=ALU.add), modvB.rs() + bmodB.rs(), modvB.rs())
        mv4 = modvB.ap.rearrange("p (l j c) w -> p l j c w", l=nl, j=6)
        gv4 = gvB.ap.rearrange("p (l q c) -> p l q c", l=nl, q=4)
        sh4 = shv[:, :].rearrange("p (l q c w) -> p l q c w", l=nl, q=2, c=NCH)
        gs4 = gsv[:, :].rearrange("p (l q c w) -> p l q c w", l=nl, q=2, c=NCH)
        gg4 = ggv[:, :].rearrange("p (l q c w) -> p l q c w", l=nl, q=2, c=NCH)
        for l in range(nl):
            for q in range(2):
                for w in range(2):
                    dve(lambda e, l=l, q=q, w=w: e.tensor_copy(out=sh4[:, l, q, :, w], in_=mv4[:, l, 3 * q, :, w]), modvB.rs(), [R_shv])
                    dve(lambda e, l=l, q=q, w=w: e.scalar_tensor_tensor(
                        out=gs4[:, l, q, :, w], in0=mv4[:, l, 1 + 3 * q, :, w], scalar=1.0, in1=gv4[:, l, 2 * q, :], op0=ALU.add, op1=ALU.mult),
                        modvB.rs() + gvB.rs(), [R_gsv])
                    dve(lambda e, l=l, q=q, w=w: e.tensor_tensor(
                        out=gg4[:, l, q, :, w], in0=mv4[:, l, 2 + 3 * q, :, w], in1=gv4[:, l, 2 * q + 1, :], op=ALU.mult), modvB.rs() + gvB.rs(), [R_ggv])

        def sumsq(srcB):
            for c in range(NCH):
                act(lambda e, c=c: e.activation(out=hB.ap[:, c, :], in_=srcB.ap[:, c, :], func=AF.Square), srcB.rows(c), hB.rows(c))
                last = (c == NCH - 1)
                for ti, (t0, tn) in enumerate(TILES):
                    pe(lambda e, ti=ti, t0=t0, tn=tn, c=c, last=last:
                       e.matmul(banks[ti][:, 0:tn], ones_b[:, :], hB.ap[:, c, t0:t0 + tn], start=(c == 0), stop=last),
                       reads=hB.rows(c) + [R_ones], writes=[R_bank[ti]], inc=(ti == 2))
            for ti, (t0, tn) in enumerate(TILES):
                act(lambda e, ti=ti, t0=t0, tn=tn: e.activation(out=rstd[:, t0:t0 + tn], in_=banks[ti][:, 0:tn], func=AF.Sqrt,
                                                                bias=cst[:, 0:1], scale=1.0 / D), [R_cst], [R_bank[ti], R_rstd])
            dve(lambda e: e.reciprocal(out=rstd[:, :], in_=rstd[:, :]), [R_rstd], [R_rstd])

        def norm_mod(l, q):
            sumsq(xB)
            for c in range(NCH):
                dve(lambda e, c=c: e.tensor_tensor(out=bigB.ap[:, c, :], in0=xB.ap[:, c, :], in1=rstd[:, :], op=ALU.mult),
                    xB.rows(c) + [R_rstd], bigB.rows(c))
                for w, (t0, tn) in enumerate([(0, T), (T, TS)]):
                    act(lambda e, c=c, w=w, t0=t0, tn=tn: e.activation(
                        out=hB.ap[:, c, t0:t0 + tn], in_=bigB.ap[:, c, t0:t0 + tn], func=AF.Identity,
                        bias=sh4[:, l, q, c, w:w + 1], scale=gs4[:, l, q, c, w:w + 1]),
                        bigB.rows(c) + [R_shv, R_gsv], hB.rows(c))

        def post_norm_add(l, q):
            sumsq(bigB)
            for c in range(NCH):
                dve(lambda e, c=c: e.tensor_tensor(out=bigB.ap[:, c, :], in0=bigB.ap[:, c, :], in1=rstd[:, :], op=ALU.mult),
                    bigB.rows(c) + [R_rstd], bigB.rows(c))
                for w, (t0, tn) in enumerate([(0, T), (T, TS)]):
                    dve(lambda e, c=c, w=w, t0=t0, tn=tn: e.scalar_tensor_tensor(
                        out=xB.ap[:, c, t0:t0 + tn], in0=bigB.ap[:, c, t0:t0 + tn], scalar=gg4[:, l, q, c, w:w + 1],
                        in1=xB.ap[:, c, t0:t0 + tn], op0=ALU.mult, op1=ALU.add),
                        bigB.rows(c) + [R_ggv] + xB.rows(c), xB.rows(c))

        def spill_x():
            sp_dma(xs_t.ap(), xB.ap.rearrange("p c t -> p (c t)"), xB.rs(), [R_xs])

        def reload_x():
            sp_dma(xB.ap.rearrange("p c t -> p (c t)"), xs_t.ap(), [R_xs], xB.rs())

        def attend_front(cx):
            qbd_ap, qbd_res, nq2, segs, bias_ap, bias_res, sink_col, sink_res = cx["qbd_ap"], cx["qbd_res"], cx["nq2"], cx["segs"], cx["bias_ap"], cx["bias_res"], cx["sink_col"], cx["sink_res"]
            par = cx["par"]
            Pfull, P_res = cx["Pfull"], cx["P_res"]
            Sp = Sps[par]
            Sres = R_SpsL[par]
            R_sm = R_smp[par]
            c0s = 4 * par
            col = 0
            mms = []
            for (kap, kres, masked, n) in segs:
                c0 = 0
                while c0 < n:
                    cn = min(n - c0, 512 - (col % 512))
                    kp = kap[:, c0:c0 + cn]
                    has_b = bias_ap is not None
                    mms.append((lambda e, kp=kp, col=col, cn=cn, has_b=has_b, masked=masked: e.matmul(Sp[0:nq2, col:col + cn], qbd_ap, kp, start=True, stop=not (has_b or masked)),
                                qbd_res + kres))
                    if has_b:
                        mms.append((lambda e, col=col, cn=cn, masked=masked: e.matmul(Sp[0:nq2, col:col + cn], ident_b[0:nq2, 0:nq2], bias_ap[:, col:col + cn], start=False, stop=not masked),
                                    bias_res + [R_ident]))
                    if masked:
                        mms.append((lambda e, col=col, cn=cn: e.matmul(Sp[0:nq2, col:col + cn], ones_b[0:1, 0:nq2], maskrow[0:1, 0:cn], start=False, stop=True),
                                    [R_ones, R_maskrow]))
                    col += cn
                    c0 += cn
            ntot = col
            cx["ntot"] = ntot
            ne = ntot
            if sink_col is not None:
                mms.append((lambda e: e.matmul(Sp[0:nq2, ntot:ntot + 1], ident_b[0:nq2, 0:nq2], sink_col, start=True, stop=True), sink_res + [R_ident]))
                ne = ntot + 1
            for mi, (fn, rd) in enumerate(mms):
                pe(fn, rd, Sres, inc=(mi == len(mms) - 1))
            Sv = Sp[0:nq2, 0:ne]
            Pv = Pfull[:, 64:64 + ne]
            dve(lambda e: e.tensor_reduce(out=sm[0:nq2, c0s:c0s + 1], in_=Sv, axis=AX.X, op=ALU.max, negate=True), [], Sres + [R_sm])
            act(lambda e: e.activation(out=Pv, in_=Sv, func=AF.Exp, bias=sm[0:nq2, c0s:c0s + 1], scale=1.0, accum_out=sm[0:nq2, c0s + 1:c0s + 2]),
                [R_sm], Sres + P_res + [R_sm])
            dve(lambda e: e.reciprocal(out=sm[0:nq2, c0s + 3:c0s + 4], in_=sm[0:nq2, c0s + 1:c0s + 2]), [R_sm], [R_sm])
            Pn = Pfull[:, 64:64 + ntot]
            dve(lambda e: e.tensor_scalar(out=Pn, in0=Pn, scalar1=sm[0:nq2, c0s + 3:c0s + 4], scalar2=None, op0=ALU.mult), P_res + [R_sm], P_res)

        def attend_back(cx):
            nq2, nqh, Pfull, P_res, tcol0, PT_ap, PT_res, vblocks, out_ap, out_res = (cx["nq2"], cx["nqh"], cx["Pfull"], cx["P_res"], cx["tcol0"],
                                                                                      cx["PT_ap"], cx["PT_res"], cx["vblocks"], cx["out_ap"], cx["out_res"])
            ntot = cx["ntot"]
            col = tcol0
            nb = len(vblocks)
            for bi, (vap, vres, nk) in enumerate(vblocks):
                pe(lambda e, col=col, nk=nk, bi=bi: e.transpose(out=PTb16[0:nk, bi * 128:bi * 128 + nq2], in_=Pfull[:, col:col + nk],
                                                               identity=ident_b[0:nq2, 0:nq2]),
                   P_res + [R_ident], [R_bank[3]], inc=(bi == nb - 1))
                col += nk
            assert col == 64 + ntot, (col, ntot, tcol0)
            act(lambda e: e.activation(out=PT_ap[:, 0:nb, 0:nq2], in_=PTb16[:, 0:nb * 128].rearrange("p (b q) -> p b q", q=128)[:, :, 0:nq2], func=AF.Identity),
                [], [R_bank[3]] + PT_res)
            for bi, (vap, vres, nk) in enumerate(vblocks):
                pe(lambda e, vap=vap, nk=nk, bi=bi: e.matmul(banks[2][:, 128:128 + nq2], vap, PT_ap[0:nk, bi, 0:nq2], start=(bi == 0), stop=(bi == nb - 1)),
                   vres + PT_res, [R_bank[2]], inc=(bi == nb - 1))
            act(lambda e: e.activation(out=out_ap[0:64, :], in_=banks[2][0:64, 128:128 + nqh], func=AF.Identity), [], [R_bank[2]] + out_res)
            act(lambda e: e.activation(out=out_ap[64:128, :], in_=banks[2][64:128, 128 + nqh:128 + 2 * nqh], func=AF.Identity), [], [R_bank[2]] + out_res)

        def run_attends(items):
            for i, cx in enumerate(items):
                cx["par"] = i % 2
            if items:
                attend_front(items[0])
            for i in range(len(items)):
                if i + 1 < len(items):
                    attend_front(items[i + 1])
                attend_back(items[i])


        def prompt_window(c, nprev, KX, KhX, kidx, halo_len, vfn):
            s = (c - nprev) * 64
            e_ = (c + 1) * 64
            segs = []
            if s < 0:
                segs.append((KhX.ap[:, kidx, halo_len + s:halo_len], KhX.rows(kidx), True, -s))
            lo = max(s, 0)
            segs.append((KX.ap[:, kidx, lo:e_], KX.rows(kidx), False, e_ - lo))
            if s % 128 == 0:
                tcol0, t0 = 64, s
            else:
                tcol0, t0 = 0, s - 64
            vblocks = []
            while t0 < e_:
                nk = min(128, e_ - t0)
                vblocks.append(vfn(t0, nk))
                t0 += 128
            return segs, vblocks, tcol0

        lr4 = lrup_s[:, :].rearrange("p (l c e) -> p l c e", l=nl, c=8)
        nls4 = nls_s[:, :].rearrange("p (l c e) -> p l c e", l=nl, c=8)
        wri4 = wri_s[:, :].rearrange("p (c g m) -> p c g m", c=8, g=2)
        sink3 = sinkb[:, :].rearrange("p (l q) -> p l q", l=nl)
        sinks3 = sinksb[:, :].rearrange("p (l q) -> p l q", l=nl)
        pref3 = pref[:, :].rearrange("p (c t) -> p c t", t=3)
        xbt3 = xbt[:, :].rearrange("p (c t) -> p c t", t=3)
        stc4 = stc_s[:, :].rearrange("p (l c t) -> p l c t", l=nl, c=8)
        sth3 = sth_s[:, :].rearrange("p (l c) -> p l c", l=nl)
        flip = [0]
        x1 = xb1_t.ap()
        y1 = yb1_t.ap()

        norm_mod(0, 0)
        for l in range(nl):
            spill_x()
            for pr in range(4):
                pool_dma(KsC.ap[:, pr, 0:512], cbk_d[(l * 4 + pr) * 128:(l * 4 + pr + 1) * 128, :], [], KsC.rows(pr))
                pool_dma(VsC.ap[:, pr, :], cbv_d[l * 512 + pr * 128:l * 512 + (pr + 1) * 128, :], [], VsC.rows(pr))
            for kh in range(2):
                pool_dma(KsA.ap[:, kh, 0:128], cswak_d[(l * 2 + kh) * 128:(l * 2 + kh + 1) * 128, :], [], KsA.rows(kh))
            pool_dma(VsA.ap[:, 0, :], cswav_d[l * 128:(l + 1) * 128, :], [], VsA.rows(0))
            pool_dma(wri_s[:, :].rearrange("p (n m) -> p n m", m=128),
                     wri_d[l * 2048:(l + 1) * 2048, :].rearrange("(n kk) m -> kk n m", kk=128), [], [R_wri])

            for b in range(4):
                sl = w_next()
                w16 = sl.ap.rearrange("p (c n) -> p c n", c=16)
                for mm in range(2):
                    c = b * 2 + mm
                    bk = 3 + (c % 2)
                    for kc in range(16):
                        last = kc == 15
                        pe(lambda e, bk=bk, w16=w16, kc=kc, mm=mm, last=last:
                           e.matmul(banks[bk][:, 0:3], w16[:, kc, mm * 128:(mm + 1) * 128], hB.ap[:, kc, T - 3:T], start=(kc == 0), stop=last),
                           reads=sl.rs() + hB.rows(kc), writes=[R_bank[bk]], inc=last)
                    dve(lambda e, bk=bk, c=c: e.tensor_copy(out=xbt3[:, c, :], in_=banks[bk][:, 0:3]), [], [R_bank[bk], R_xbt])
            sp_dma(x1[1408:1536, 0:48], xbt[:, :].bitcast(BF16), [R_xbt], [R_xb1])
            dve(lambda e, l=l: e.tensor_copy(out=cl_s[:, l * 24:(l + 1) * 24], in_=xbt[:, :]), [R_xbt], [R_cl])

            def k_chunk(sl, mm, dstB, dsti, halo_row0, out_d, out_r0, tail0, s_dst, s_dst_res, s_out, s_r0):
                w16 = sl.ap.rearrange("p (c n) -> p c n", c=16)
                gemm_fm(w16, mm * 128, hB, 16, (0, 1, 2), sl.rs())
                act(lambda e: e.activation(out=dstB.ap[:, dsti, 0:512], in_=banks[0][:, :], func=AF.Identity),
                    [], [R_bank[0]] + dstB.rs(dsti * T, dsti * T + 512))
                sg = stg[flip[0] % 2]
                flip[0] += 1
                dve(lambda e: e.tensor_copy(out=dstB.ap[:, dsti, 512:T], in_=banks[1][:, :]),
                    [], [R_bank[1]] + dstB.rs(dsti * T + 512, dsti * T + T))
                act(lambda e: e.activation(out=sg.ap, in_=banks[1][:, :], func=AF.Identity), [], [R_bank[1]] + sg.rs())
                ntail = 512 - tail0
                sp_dma(out_d[out_r0:out_r0 + 128, 0:ntail], sg.ap[:, tail0:512], sg.rs(), [R_out])
                sp_dma(x1[halo_row0:halo_row0 + 128, 0:ntail], dstB.ap[:, dsti, 512 + tail0:T], dstB.rs(dsti * T + 512, dsti * T + T), [R_xb1])
                dve(lambda e: e.tensor_copy(out=stg_s.ap[:, 0:TS], in_=banks[2][:, 0:TS]), [], [R_bank[2]] + stg_s.rs())
                dve(lambda e: e.tensor_copy(out=s_dst, in_=stg_s.ap[:, 0:TS]), stg_s.rs(), s_dst_res)
                sp_dma(s_out[s_r0:s_r0 + 128, :], stg_s.ap[:, 0:TS], stg_s.rs(), [R_out])

            for b in range(2):
                sl = w_next()
                for mm in range(2):
                    pr = b * 2 + mm
                    k_chunk(sl, mm, KC, pr, pr * 128, kc_o, (l * 4 + pr) * 128, 0, KsC.ap[:, pr, 512:528], KsC.rows(pr), kcs_o, (l * 4 + pr) * 128)
            sl = w_next()
            for mm in range(2):
                k_chunk(sl, mm, KA, mm, 1024 + mm * 128, ka_o, (l * 2 + mm) * 128, 384, KsA.ap[:, mm, 128:144], KsA.rows(mm), kas_o, (l * 2 + mm) * 128)

            def v_block(sl, dstB, col0, out_d, out_r0, halo_r0, tail_tb, s_dstB, s_blk, s_out, s_r0):
                ncol = 256
                w16 = sl.ap.rearrange("p (c n) -> p c n", c=16)
                for tb in range(9):
                    bk = 3 + (tb % 2)
                    m = 128 if tb < 8 else TS
                    t0 = tb * 128
                    for kc in range(16):
                        last = kc == 15
                        pe(lambda e, bk=bk, w16=w16, kc=kc, t0=t0, m=m, last=last:
                           e.matmul(banks[bk][0:m, 0:ncol], hB.ap[:, kc, t0:t0 + m], w16[:, kc, 0:ncol], start=(kc == 0), stop=last),
                           reads=sl.rs() + hB.rows(kc), writes=[R_bank[bk]], inc=last)
                    if tb < 8:
                        act(lambda e, bk=bk, tb=tb: e.activation(out=dstB.ap[:, tb, col0:col0 + ncol], in_=banks[bk][:, 0:ncol], func=AF.Identity),
                            [], [R_bank[bk]] + dstB.rows(tb))
                        if tb >= tail_tb:
                            sg = stg[flip[0] % 2]
                            flip[0] += 1
                            dve(lambda e, bk=bk, sg=sg: e.tensor_copy(out=sg.ap[:, 0:ncol], in_=banks[bk][:, 0:ncol]), [], [R_bank[bk]] + sg.rs())
                            r = (tb - tail_tb) * 128
                            sp_dma(out_d[out_r0 + r:out_r0 + r + 128, col0:col0 + ncol], sg.ap[:, 0:ncol], sg.rs(), [R_out])
                            sp_dma(x1[halo_r0 + r:halo_r0 + r + 128, col0:col0 + ncol], dstB.ap[:, tb, col0:col0 + ncol], dstB.rows(tb), [R_xb1])
                    else:
                        dve(lambda e, bk=bk: e.tensor_copy(out=stg_s.ap[0:TS, 0:ncol], in_=banks[bk][0:TS, 0:ncol]), [], [R_bank[bk]] + stg_s.rs())
                        dve(lambda e: e.tensor_copy(out=s_dstB.ap[0:TS, s_blk, col0:col0 + ncol], in_=stg_s.ap[0:TS, 0:ncol]),
                            stg_s.rs(), s_dstB.rows(s_blk))
                        sp_dma(s_out[s_r0:s_r0 + TS, col0:col0 + ncol], stg_s.ap[0:TS, 0:ncol], stg_s.rs(), [R_out])

            for b in range(2):
                sl = w_next()
                v_block(sl, VC, b * 256, vc_o, l * 512, 512, 4, VsC, 4, vcs_o, l * TS)
            sl = w_next()
            v_block(sl, VA, 0, va_o, l * 128, 1280, 7, VsA, 1, vas_o, l * TS)

            k.custom("pool", lambda e: e.collective_compute("AllGather", ALU.bypass, replica_groups=pair_groups,
                                                            ins=[xb1_t.ap().opt()], outs=[yb1_t.ap().opt()]),
                     cc_sem, reads=[R_xb1], writes=[R_yb1])
            for pr in range(4):
                sp_dma(KhC.ap[:, pr, :], y1[pr * 128:(pr + 1) * 128, :], [R_yb1], KhC.rows(pr))
                sp_dma(VhC.ap[:, pr, :], y1[512 + pr * 128:512 + (pr + 1) * 128, :], [R_yb1], VhC.rows(pr))
            for kh in range(2):
                sp_dma(KhA.ap[:, kh, :], y1[1024 + kh * 128:1024 + (kh + 1) * 128, 0:128], [R_yb1], KhA.rows(kh))
            sp_dma(VhA.ap[:, 0, :], y1[1280:1408, 0:256], [R_yb1], VhA.rows(0))
            sp_dma(pref[:, :].bitcast(BF16), y1[1408:1536, 0:48], [R_yb1], [R_pref])
            dve(lambda e: e.tensor_scalar(out=pref[:, :], in0=pref[:, :], scalar1=flags_s[:, 1:2], scalar2=None, op0=ALU.mult),
                [R_pref, R_flags], [R_pref])

            for c in range(8):
                sl = w_next()
                w16 = sl.ap.rearrange("p (c n) -> p c n", c=16)
                gemm_fm(w16, 0, hB, 16, (0, 1, 2), sl.rs())
                gemm_fm(w16, 128, hB, 16, (3, 4, 5), sl.rs())
                dve(lambda e, c=c: e.tensor_copy(out=xbuf.ap[:, 0:3], in_=pref3[:, c, :]), [R_pref], xbuf.rs(0, 3))
                act(lambda e: e.activation(out=xbuf.ap[:, 3:515], in_=banks[0][:, :], func=AF.Identity), [], [R_bank[0]] + xbuf.rs(3, 515))
                dve(lambda e: e.tensor_copy(out=xbuf.ap[:, 515:1027], in_=banks[1][:, :]), [], [R_bank[1]] + xbuf.rs(515, 1027))
                dve(lambda e, c=c, l=l: e.tensor_copy(out=sx[:, 0:3], in_=stc4[:, l, c, :]), [R_stc], [R_sx])
                dve(lambda e: e.tensor_copy(out=sx[:, 3:19], in_=banks[2][:, 0:TS]), [], [R_bank[2], R_sx])
                cw = lr4[:, l, c, :]
                ns = nls4[:, l, c, :]
                wr = wri4[:, c, 0, :]
                wi = wri4[:, c, 1, :]
                lanes = [
                    dict(n=512, src=xbuf.ap[:, 0:515], src_res=xbuf.rs(0, 515), xc=lr_xc.ap, xc_res=lr_xc.rs(), xcb=lr_xcb.ap, xcb_res=lr_xcb.rs(),
                         r=lr_r.ap, r_res=lr_r.rs(), i=lr_i.ap, i_res=lr_i.rs(), a=lr_a.ap, a_res=lr_a.rs(), gbk=3, g0=0),
                    dict(n=512, src=xbuf.ap[:, 512:1027], src_res=xbuf.rs(512, 1027), xc=lr2_xc.ap, xc_res=lr2_xc.rs(), xcb=lr2_xcb.ap, xcb_res=lr2_xcb.rs(),
                         r=lr2_r.ap, r_res=lr2_r.rs(), i=lr2_i.ap, i_res=lr2_i.rs(), a=lr2_a.ap, a_res=lr2_a.rs(), gbk=4, g0=512),
                    dict(n=TS, src=sx[:, 0:19], src_res=[R_sx], xc=sxc[:, :], xc_res=[R_sxc], xcb=sxcb[:, :], xcb_res=[R_sxcb],
                         r=sr[:, :], r_res=[R_sr], i=si[:, :], i_res=[R_si], a=sa[:, :], a_res=[R_sa], gbk=5, g0=T),
                ]
                for ln in lanes:
                    act(lambda e, ln=ln: e.activation(out=ln["r"], in_=banks[ln["gbk"]][:, 0:ln["n"]], func=AF.Square), [], [R_bank[ln["gbk"]]] + ln["r_res"])
                for ln in lanes:
                    dve(lambda e, ln=ln: e.tensor_scalar(out=ln["r"], in0=ln["r"], scalar1=0.044715, scalar2=1.0, op0=ALU.mult, op1=ALU.add), ln["r_res"], ln["r_res"])
                    dve(lambda e, ln=ln: e.tensor_tensor(out=ln["r"], in0=ln["r"], in1=banks[ln["gbk"]][:, 0:ln["n"]], op=ALU.mult), ln["r_res"], [R_bank[ln["gbk"]]] + ln["r_res"])
                for ln in lanes:
                    n = ln["n"]
                    dve(lambda e, ln=ln, n=n, cw=cw: e.tensor_scalar(out=ln["xc"], in0=ln["src"][:, 0:n], scalar1=cw[:, 0:1], scalar2=cw[:, 4:5], op0=ALU.mult, op1=ALU.add),
                        ln["src_res"] + [R_lrup], ln["xc_res"])
                    for j in range(1, 4):
                        dve(lambda e, ln=ln, n=n, j=j, cw=cw: e.scalar_tensor_tensor(out=ln["xc"], in0=ln["src"][:, j:j + n], scalar=cw[:, j:j + 1], in1=ln["xc"], op0=ALU.mult, op1=ALU.add),
                            ln["src_res"] + [R_lrup] + ln["xc_res"], ln["xc_res"])
                for ln in lanes:
                    act(lambda e, ln=ln: e.activation(out=ln["r"], in_=ln["r"], func=AF.Sigmoid, scale=1.5957691216), ln["r_res"], ln["r_res"])
                for ln in lanes:
                    dve(lambda e, ln=ln: e.tensor_tensor(out=lr_g.ap[:, ln["g0"]:ln["g0"] + ln["n"]], in0=ln["r"], in1=banks[ln["gbk"]][:, 0:ln["n"]], op=ALU.mult),
                        ln["r_res"], [R_bank[ln["gbk"]]] + lr_g.rs(ln["g0"], ln["g0"] + ln["n"]))
                for ln in lanes:
                    act(lambda e, ln=ln: e.activation(out=ln["xcb"], in_=ln["xc"], func=AF.Identity), ln["xc_res"], ln["xcb_res"])
                for ln in lanes:
                    n = ln["n"]
                    pe(lambda e, ln=ln, n=n, wr=wr: e.matmul(S2[:, 0:n], wr, ln["xcb"], start=True, stop=True), [R_wri] + ln["xcb_res"], R_S2L, inc=False)
                    pe(lambda e, ln=ln, n=n, wi=wi: e.matmul(S2[:, 512:512 + n], wi, ln["xcb"], start=True, stop=True), [R_wri] + ln["xcb_res"], R_S2L)
                    act(lambda e, ln=ln, n=n, cw=cw: e.activation(out=ln["r"], in_=S2[:, 0:n], func=AF.Sigmoid, bias=cw[:, 5:6], scale=1.0), [R_lrup], R_S2L + ln["r_res"])
                    act(lambda e, ln=ln, n=n, cw=cw: e.activation(out=ln["i"], in_=S2[:, 512:512 + n], func=AF.Sigmoid, bias=cw[:, 6:7], scale=1.0), [R_lrup], R_S2L + ln["i_res"])
                for ln in lanes:
                    act(lambda e, ln=ln, ns=ns: e.activation(out=ln["a"], in_=ln["r"], func=AF.Exp, scale=ns[:, 0:1]), ln["r_res"] + [R_nls], ln["a_res"])
                    act(lambda e, ln=ln, ns=ns: e.activation(out=ln["r"], in_=ln["r"], func=AF.Exp, scale=ns[:, 1:2]), ln["r_res"] + [R_nls], ln["r_res"])
                for ln in lanes:
                    act(lambda e, ln=ln: e.activation(out=ln["r"], in_=ln["r"], func=AF.Sqrt, bias=cst[:, 1:2], scale=-1.0), ln["r_res"] + [R_cst], ln["r_res"])
                for ln in lanes:
                    dve(lambda e, ln=ln: e.tensor_tensor(out=ln["i"], in0=ln["i"], in1=ln["r"], op=ALU.mult), ln["i_res"] + ln["r_res"], ln["i_res"])
                    dve(lambda e, ln=ln: e.tensor_tensor(out=ln["i"], in0=ln["i"], in1=ln["xc"], op=ALU.mult), ln["i_res"] + ln["xc_res"], ln["i_res"])
                for ti in range(2):
                    ln = lanes[ti]
                    t0 = ti * 512
                    ini_h = 0.0 if ti == 0 else car[:, 0:1]
                    ini_p = 1.0 if ti == 0 else car[:, 1:2]
                    dve(lambda e, ln=ln, ini_h=ini_h: e.tensor_tensor_scan(out=lr_h.ap, data0=ln["a"], data1=ln["i"], initial=ini_h, op0=ALU.mult, op1=ALU.add),
                        ln["a_res"] + ln["i_res"] + [R_car], lr_h.rs())
                    dve(lambda e, ln=ln, ini_p=ini_p: e.tensor_tensor_scan(out=lr_p.ap, data0=ln["a"], data1=zeros_bc, initial=ini_p, op0=ALU.mult, op1=ALU.add),
                        ln["a_res"] + [R_cst, R_car], lr_p.rs())
                    dve(lambda e: e.tensor_copy(out=car[:, 0:1], in_=lr_h.ap[:, 511:512]), lr_h.rs(), [R_car])
                    dve(lambda e: e.tensor_copy(out=car[:, 1:2], in_=lr_p.ap[:, 511:512]), lr_p.rs(), [R_car])
                    dve(lambda e, c=c, t0=t0: e.tensor_tensor(out=mixB.ap[:, 4 + c, t0:t0 + 512], in0=lr_h.ap, in1=lr_g.ap[:, t0:t0 + 512], op=ALU.mult),
                        lr_h.rs() + lr_g.rs(t0, t0 + 512), mixB.rs((4 + c) * NT + t0, (4 + c) * NT + t0 + 512))
                    dve(lambda e, c=c, t0=t0: e.tensor_tensor(out=PgB.ap[:, c, t0:t0 + 512], in0=lr_p.ap, in1=lr_g.ap[:, t0:t0 + 512], op=ALU.mult),
                        lr_p.rs() + lr_g.rs(t0, t0 + 512), PgB.rs(c * T + t0, c * T + t0 + 512))
                dve(lambda e, c=c: e.tensor_copy(out=h0l[:, c:c + 1], in_=car[:, 0:1]), [R_car], [R_h0l])
                dve(lambda e, c=c: e.tensor_copy(out=Pl[:, c:c + 1], in_=car[:, 1:2]), [R_car], [R_Pl])
                dve(lambda e, c=c, l=l: e.tensor_tensor_scan(out=sh[:, :], data0=sa[:, :], data1=si[:, :], initial=sth3[:, l, c:c + 1], op0=ALU.mult, op1=ALU.add),
                    [R_sa, R_si, R_sth], [R_sh])
                dve(lambda e, c=c: e.tensor_tensor(out=mixB.ap[:, 4 + c, T:NT], in0=sh[:, :], in1=lr_g.ap[:, T:NT], op=ALU.mult),
                    [R_sh] + lr_g.rs(T, NT), mixB.rs((4 + c) * NT + T, (4 + c + 1) * NT))
                dve(lambda e, c=c, l=l: e.tensor_copy(out=hls_s[:, l * 8 + c:l * 8 + c + 1], in_=sh[:, TS - 1:TS]), [R_sh], [R_hls])
                dve(lambda e, c=c, l=l: e.tensor_copy(out=cls_s[:, (l * 8 + c) * 3:(l * 8 + c) * 3 + 3], in_=sx[:, 16:19]), [R_sx], [R_cls])
            dve(lambda e: e.memset(QBD.ap, 0.0), [], QBD.rs())
            for pb in P_sb:
                dve(lambda e, pb=pb: e.memset(pb.ap[:, 0:64], 0.0), [], pb.rs(0, 64))


            sp_dma(xb2_t.ap(), h0l[:, :], [R_h0l], [R_xb2])
            k.custom("pool", lambda e: e.collective_compute("AllGather", ALU.bypass, replica_groups=pair_groups,
                                                            ins=[xb2_t.ap().opt()], outs=[yb2_t.ap().opt()]),
                     cc_sem, reads=[R_xb2], writes=[R_yb2])
            sp_dma(hin[:, :], yb2_t.ap()[0:128, :], [R_yb2], [R_hin])
            dve(lambda e: e.tensor_scalar(out=hin[:, :], in0=hin[:, :], scalar1=flags_s[:, 1:2], scalar2=None, op0=ALU.mult), [R_hin, R_flags], [R_hin])

            for qb in range(4):
                sl = w_next()
                w16 = sl.ap.rearrange("p (c n) -> p c n", c=16)
                for mm in range(2):
                    qc = (qb % 2) * 2 + mm
                    is_band = qb >= 2
                    gemm_fm(w16, mm * 128, hB, 16, (0, 1, 2), sl.rs())
                    q4 = QBD.ap.rearrange("p c (h q) -> p c h q", h=2)
                    for ti in range(2):
                        for hh in range(2):
                            act(lambda e, ti=ti, hh=hh, q4=q4: e.activation(
                                out=q4[64 * hh:64 * hh + 64, ti * 8:(ti + 1) * 8, hh, :],
                                in_=banks[ti][64 * hh:64 * hh + 64, :].rearrange("p (c q) -> p c q", q=64),
                                func=AF.Identity, scale=0.125), [], [R_bank[ti]] + QBD.rows(ti * 8, ti * 8 + 8))
                    qs4 = QBDs[:, :].rearrange("p (h q) -> p h q", h=2)
                    for hh in range(2):
                        act(lambda e, hh=hh: e.activation(out=qs4[64 * hh:64 * hh + 64, hh, :], in_=banks[2][64 * hh:64 * hh + 64, 0:TS],
                                                          func=AF.Identity, scale=0.125), [], [R_bank[2], R_QBDs])
                    def mk_item(i, qbd_ap, qbd_res, nq2, nqh, segs, bias_ap, bias_res, sink_col, sink_res, tcol0, vblocks, out_ap, out_res):
                        par = i % 2
                        return dict(qbd_ap=qbd_ap, qbd_res=qbd_res, nq2=nq2, nqh=nqh, segs=segs, bias_ap=bias_ap, bias_res=bias_res,
                                    sink_col=sink_col, sink_res=sink_res, tcol0=tcol0, vblocks=vblocks, out_ap=out_ap, out_res=out_res,
                                    S_ap=(S_sb[par].ap if nq2 == 128 else S_sb[par].ap[0:32, 0:528]), S_res=S_sb[par].rs(),
                                    Pfull=(P_sb[par].ap if nq2 == 128 else P_sb[par].ap[0:32, :]), P_res=P_sb[par].rs(),
                                    PT_ap=(PT_sb[par].ap if nq2 == 128 else PT_sb[par].ap[:, :, 0:32]), PT_res=PT_sb[par].rs())

                    items = []
                    if is_band:
                        pr = qc
                        r0 = (l * 4 + pr) * 128
                        pool_dma(biaspB.ap, biasp_d[r0:r0 + 128, :], [], biaspB.rs())
                        r0s = (l * 4 + pr) * 32
                        pool_dma(biass_s[:, :], biass_d[r0s:r0s + 32, :], [], [R_biass])
                        mchunk = 12 + pr

                        def vfn(t0, nk, pr=pr):
                            gb = (t0 + 512) // 128
                            if gb < 4:
                                return (VhC.ap[0:nk, gb, pr * 128:(pr + 1) * 128], VhC.rows(gb), nk)
                            return (VC.ap[0:nk, gb - 4, pr * 128:(pr + 1) * 128], VC.rows(gb - 4), nk)

                        for c in range(16):
                            segs, vblocks, tcol0 = prompt_window(c, 8, KC, KhC, pr, 512, vfn)
                            items.append(mk_item(c, QBD.ap[:, c, :], QBD.rows(c), 128, 64, segs, biaspB.ap, biaspB.rs(), None, [], tcol0, vblocks,
                                                 mixB.ap[:, mchunk, c * 64:(c + 1) * 64], mixB.rs(mchunk * NT + c * 64, mchunk * NT + (c + 1) * 64)))
                        segs = [(KsC.ap[:, pr, 0:528], KsC.rows(pr), False, 528)]
                        vblocks = [(VsC.ap[:, b4, pr * 128:(pr + 1) * 128], VsC.rows(b4), 128) for b4 in range(4)]
                        vblocks.append((VsC.ap[0:TS, 4, pr * 128:(pr + 1) * 128], VsC.rows(4), TS))
                        items.append(mk_item(16, QBDs[:, :], [R_QBDs], 32, TS, segs, biass_s[:, :], [R_biass], None, [], 64, vblocks,
                                             mixB.ap[:, mchunk, T:NT], mixB.rs(mchunk * NT + T, (mchunk + 1) * NT)))
                    else:
                        kh = qc // 2
                        mchunk = qc

                        def vfn(t0, nk, kh=kh):
                            if t0 < 0:
                                return (VhA.ap[0:nk, 0, kh * 128:(kh + 1) * 128], VhA.rows(0), nk)
                            return (VA.ap[0:nk, t0 // 128, kh * 128:(kh + 1) * 128], VA.rows(t0 // 128), nk)

                        for c in range(16):
                            segs, vblocks, tcol0 = prompt_window(c, 2, KA, KhA, kh, 128, vfn)
                            items.append(mk_item(c, QBD.ap[:, c, :], QBD.rows(c), 128, 64, segs, None, [], sink3[:, l, qc:qc + 1], [R_sinkb], tcol0, vblocks,
                                                 mixB.ap[:, mchunk, c * 64:(c + 1) * 64], mixB.rs(mchunk * NT + c * 64, mchunk * NT + (c + 1) * 64)))
                        segs = [(KsA.ap[:, kh, 0:144], KsA.rows(kh), False, 144)]
                        vblocks = [(VsA.ap[:, 0, kh * 128:(kh + 1) * 128], VsA.rows(0), 128),
                                   (VsA.ap[0:TS, 1, kh * 128:(kh + 1) * 128], VsA.rows(1), TS)]
                        items.append(mk_item(16, QBDs[:, :], [R_QBDs], 32, TS, segs, None, [], sinks3[:, l, qc:qc + 1], [R_sinksb], 64, vblocks,
                                             mixB.ap[:, mchunk, T:NT], mixB.rs(mchunk * NT + T, (mchunk + 1) * NT)))
                    run_attends(items)

            for c in range(8):
                dve(lambda e, c=c: e.scalar_tensor_tensor(out=mixB.ap[:, 4 + c, 0:T], in0=PgB.ap[:, c, :], scalar=hin[:, c:c + 1],
                                                          in1=mixB.ap[:, 4 + c, 0:T], op0=ALU.mult, op1=ALU.add),
                    PgB.rows(c) + [R_hin] + mixB.rs((4 + c) * NT, (4 + c) * NT + T), mixB.rs((4 + c) * NT, (4 + c) * NT + T))
            dve(lambda e: e.tensor_tensor(out=hlt[:, :], in0=Pl[:, :], in1=hin[:, :], op=ALU.mult), [R_Pl, R_hin], [R_hlt])
            dve(lambda e, l=l: e.tensor_tensor(out=hl_s[:, l * 8:(l + 1) * 8], in0=hlt[:, :], in1=h0l[:, :], op=ALU.add), [R_hlt, R_h0l], [R_hl])

            for b in range(8):
                sl = w_next()
                w16 = sl.ap.rearrange("p (c n) -> p c n", c=16)
                for mm in range(2):
                    m = b * 2 + mm
                    bset = (0, 1, 2) if m % 2 == 0 else (3, 4, 5)
                    gemm_fm(w16, mm * 128, mixB, 16, bset, sl.rs())
                    act(lambda e, m=m, bset=bset: e.activation(out=bigB.ap[:, m, 0:512], in_=banks[bset[0]][:, :], func=AF.Identity),
                        [], [R_bank[bset[0]]] + bigB.rs(m * NT, m * NT + 512))
                    dve(lambda e, m=m, bset=bset: e.tensor_copy(out=bigB.ap[:, m, 512:T], in_=banks[bset[1]][:, :]),
                        [], [R_bank[bset[1]]] + bigB.rs(m * NT + 512, m * NT + T))
                    dve(lambda e, m=m, bset=bset: e.tensor_copy(out=bigB.ap[:, m, T:NT], in_=banks[bset[2]][:, 0:TS]),
                        [], [R_bank[bset[2]]] + bigB.rs(m * NT + T, (m + 1) * NT))
            reload_x()
            post_norm_add(l, 0)
            norm_mod(l, 1)
            spill_x()
            rflip = 0
            for hb in range(8):
                for b in range(4):
                    sl = w_next()
                    w16 = sl.ap.rearrange("p (c n) -> p c n", c=16)
                    for mm in range(2):
                        j = b * 2 + mm
                        bset = (0, 1, 2) if j % 2 == 0 else (3, 4, 5)
                        gemm_fm(w16, mm * 128, hB, 16, bset, sl.rs())
                        for ti, (t0, tn) in enumerate(TILES):
                            bk = bset[ti]
                            rs_ = relu_s[rflip % 2]
                            rflip += 1
                            act(lambda e, bk=bk, tn=tn, rs_=rs_: e.activation(out=rs_.ap[:, 0:tn], in_=banks[bk][:, 0:tn], func=AF.Relu),
                                [], [R_bank[bk]] + rs_.rs())
                            dve(lambda e, j=j, t0=t0, tn=tn, rs_=rs_: e.tensor_tensor(out=uB.ap[:, j, t0:t0 + tn], in0=rs_.ap[:, 0:tn], in1=rs_.ap[:, 0:tn], op=ALU.mult),
                                rs_.rs(), uB.rs(j * NT + t0, j * NT + t0 + tn))
                for mg in range(4):
                    sl = w_next()
                    w8 = sl.ap.rearrange("p (c n) -> p c n", c=8)
                    for mm in range(4):
                        m = mg * 4 + mm
                        bset = (0, 1, 2) if m % 2 == 0 else (3, 4, 5)
                        gemm_fm(w8, mm * 128, uB, 8, bset, sl.rs())
                        for ti, (t0, tn) in enumerate(TILES):
                            bk = bset[ti]
                            dst = bigB.ap[:, m, t0:t0 + tn]
                            dres = bigB.rs(m * NT + t0, m * NT + t0 + tn)
                            if hb == 0:
                                if ti == 0:
                                    act(lambda e, bk=bk, dst=dst, tn=tn: e.activation(out=dst, in_=banks[bk][:, 0:tn], func=AF.Identity), [], [R_bank[bk]] + dres)
                                else:
                                    dve(lambda e, bk=bk, dst=dst, tn=tn: e.tensor_copy(out=dst, in_=banks[bk][:, 0:tn]), [], [R_bank[bk]] + dres)
                            else:
                                dve(lambda e, bk=bk, dst=dst, tn=tn: e.tensor_tensor(out=dst, in0=banks[bk][:, 0:tn], in1=dst, op=ALU.add), [], [R_bank[bk]] + dres)
            reload_x()
            post_norm_add(l, 1)
            if l + 1 < nl:
                norm_mod(l + 1, 0)
        sp_dma(yT_o, xB.ap.rearrange("p c t -> p (c t)"), xB.rs(), [R_out])
        sp_dma(hl_o, hl_s[:, :], [R_hl], [R_out])
        sp_dma(cl_o, cl_s[:, :], [R_cl], [R_out])
        sp_dma(hls_o, hls_s[:, :], [R_hls], [R_out])
        sp_dma(cls_o, cls_s[:, :], [R_cls], [R_out])
        k.wait_all("sp", [R_out, R_xb1, R_xb2, R_xs, R_yb1, R_yb2])
        k.emit()
        build_program.stats = {n: len(E.ops) for n, E in k.eng.items()}
        build_program.stats["nsem"] = k.nsem
    return nc


_OFF = dict(qa=0, ka=512, va=640, xb=768, gb=1792, qc=2816, kc=3328, vc=3840)


def _pp(v, n):
    v = np.asarray(v, np.float32)
    lead = v.shape[:-1]
    v = v.reshape(lead + (n, 128))
    v = np.moveaxis(v, -1, 0)
    return np.ascontiguousarray(v)


def _shared_inputs(inp, nl, ncores):
    f = np.float32
    w_in = np.asarray(inp["w_in"], f)[:nl]
    cols = []
    xb0, gb0 = _OFF["xb"], _OFF["gb"]
    for i in range(4):
        cols.append(w_in[:, :, xb0 + 256 * i:xb0 + 256 * (i + 1)])
    cols.append(w_in[:, :, _OFF["kc"]:_OFF["kc"] + 512])
    ka = w_in[:, :, _OFF["ka"]:_OFF["ka"] + 128]
    cols += [ka[:, :, 0:64], ka[:, :, 0:64], ka[:, :, 64:128], ka[:, :, 64:128]]
    cols.append(w_in[:, :, _OFF["vc"]:_OFF["vc"] + 512])
    va = w_in[:, :, _OFF["va"]:_OFF["va"] + 128]
    cols += [va[:, :, 0:64], va[:, :, 0:64], va[:, :, 64:128], va[:, :, 64:128]]
    for c in range(8):
        cols.append(w_in[:, :, xb0 + 128 * c:xb0 + 128 * (c + 1)])
        cols.append(w_in[:, :, gb0 + 128 * c:gb0 + 128 * (c + 1)])
    cols.append(w_in[:, :, _OFF["qa"]:_OFF["qa"] + 512])
    cols.append(w_in[:, :, _OFF["qc"]:_OFF["qc"] + 512])
    win = np.ascontiguousarray(np.concatenate(cols, axis=2))
    assert win.shape[2] == WCOLS, win.shape
    sh = {}
    sh["win"] = win
    sh["wout"] = np.ascontiguousarray(np.asarray(inp["w_out"], f)[:nl])
    sh["wup"] = np.ascontiguousarray(np.asarray(inp["w_up"], f)[:nl])
    sh["wdn"] = np.ascontiguousarray(np.asarray(inp["w_down"], f)[:nl])
    sh["bmod"] = _pp(np.asarray(inp["b_mod"], f)[:nl], 96).reshape(128, nl * 96)
    gv = np.stack([np.asarray(inp[n], f)[:nl] for n in ("g_pre_mix", "g_post_mix", "g_pre_mlp", "g_post_mlp")], axis=1)
    sh["gv"] = _pp(gv, 16).reshape(128, nl * 4 * 16)
    cw = np.asarray(inp["lru_conv_w"], f)[:nl]
    parts = [cw[:, j] for j in range(4)] + [np.asarray(inp["lru_conv_b"], f)[:nl],
                                             np.asarray(inp["lru_b_r"], f)[:nl].reshape(nl, 1024),
                                             np.asarray(inp["lru_b_i"], f)[:nl].reshape(nl, 1024),
                                             np.asarray(inp["lru_lambda"], f)[:nl]]
    lr = np.stack(parts, axis=1)
    lr = _pp(lr, 8)
    sh["lrup"] = np.ascontiguousarray(lr.transpose(0, 1, 3, 2)).reshape(128, nl * 64)
    wri = np.zeros((nl, 8, 2, 128, 128), f)
    for gi, name in enumerate(("lru_w_r", "lru_w_i")):
        w = np.asarray(inp[name], f)[:nl]
        for c in range(8):
            wri[:, c, gi, 0:64, 0:64] = w[:, 2 * c]
            wri[:, c, gi, 64:128, 64:128] = w[:, 2 * c + 1]
    sh["wri"] = wri.reshape(nl * 16 * 128, 128)
    tab = np.asarray(inp["band_rel_bias"], f)[:nl]
    i = np.arange(64)[:, None]
    j = np.arange(576)[None, :]
    idx = np.clip(i + 512 - j, -128, 128) + 128
    bp = tab[:, :, idx]
    sh["biasp"] = np.ascontiguousarray(bp.reshape(nl, 4, 128, 576)).reshape(nl * 4 * 128, 576)
    i = np.arange(TS)[:, None]
    j = np.arange(528)[None, :]
    idx = np.clip(i + 512 - j, -128, 128) + 128
    bs = tab[:, :, idx]
    sh["biass"] = np.ascontiguousarray(bs.reshape(nl, 4, 32, 528)).reshape(nl * 4 * 32, 528)
    sk = np.asarray(inp["swa_sink"], f)[:nl]
    sink = np.zeros((128, nl, 4), f)
    sinks = np.zeros((32, nl, 4), f)
    for qc in range(4):
        for s in range(2):
            sink[64 * s:64 * s + 64, :, qc] = sk[None, :, 2 * qc + s]
            sinks[16 * s:16 * s + 16, :, qc] = sk[None, :, 2 * qc + s]
    sh["sink"] = sink.reshape(128, nl * 4)
    sh["sinks"] = sinks.reshape(32, nl * 4)
    sh["ident"] = np.eye(128, dtype=f)
    return sh


def _core_inputs(inp, sh, core, nl, ncores, cores_global):
    f = np.float32
    g = cores_global[core]
    b, half = g // 2, g % 2
    d = dict(sh)
    xp = np.asarray(inp["x_prompt"], f)[b, half * T:(half + 1) * T]
    xs = np.asarray(inp["x_sample"], f)[g]
    xT = np.concatenate([xp, xs], axis=0).T
    d["xT"] = np.ascontiguousarray(xT.reshape(16, 128, NT).transpose(1, 0, 2)).reshape(128, 16 * NT)
    call = np.concatenate([np.asarray(inp["c_prompt"], f), np.asarray(inp["c_sample"], f)], axis=0)
    d["cT"] = np.ascontiguousarray(call.T.reshape(16, 128, 12).transpose(1, 0, 2)).reshape(128, 16 * 12)
    nsh = 12288 // ncores
    d["wmod"] = np.ascontiguousarray(np.asarray(inp["w_mod"], f)[:nl, :, core * nsh:(core + 1) * nsh])
    sel = np.zeros((128, 24), f)
    sel[:, b] = 1.0
    sel[:, 12 + 4 + g] = 1.0
    d["sel"] = sel
    flags = np.zeros((128, 2), f)
    flags[:, 0] = -1e30 if half == 0 else 0.0
    flags[:, 1] = 0.0 if half == 0 else 1.0
    d["flags"] = flags
    ck = np.asarray(inp["cache_swa_k"], f)[:nl, g]
    ckT = ck.transpose(0, 2, 3, 1)
    d["cswak"] = np.ascontiguousarray(np.concatenate([ckT, ckT], axis=2)).reshape(nl * 2 * 128, 128)
    cv = np.asarray(inp["cache_swa_v"], f)[:nl, g]
    d["cswav"] = np.ascontiguousarray(np.concatenate([cv[:, :, 0], cv[:, :, 0], cv[:, :, 1], cv[:, :, 1]], axis=2)).reshape(nl * 128, 256)
    bk = np.asarray(inp["cache_band_k"], f)[:nl, g].reshape(nl, 512, 512)
    d["cbk"] = np.ascontiguousarray(bk.transpose(0, 2, 1)).reshape(nl * 512, 512)
    d["cbv"] = np.ascontiguousarray(np.asarray(inp["cache_band_v"], f)[:nl, g].reshape(nl * 512, 512))
    d["sth"] = _pp(np.asarray(inp["state_lru_h"], f)[:nl, g], 8).reshape(128, nl * 8)
    stc = np.asarray(inp["state_lru_conv"], f)[:nl, g]
    stc = _pp(stc, 8)
    d["stc"] = np.ascontiguousarray(stc.transpose(0, 1, 3, 2)).reshape(128, nl * 24)
    return d


_PROG_CACHE = {}


def _run(inp, nl=NLAYER, cores_global=None):
    if cores_global is None:
        cores_global = list(range(8))
    ncores = len(cores_global)
    key = (nl, ncores)
    if key not in _PROG_CACHE:
        _PROG_CACHE[key] = build_program(nl, ncores)
    nc = _PROG_CACHE[key]
    sh = _shared_inputs(inp, nl, ncores)
    in_maps = [_core_inputs(inp, sh, i, nl, ncores, cores_global) for i in range(ncores)]
    res = run_bass_kernel_spmd(nc, in_maps, core_ids=list(range(ncores)))
    return res.results


def _unpp(a, lead, n):
    a = np.moveaxis(a, 0, -1)
    return np.ascontiguousarray(a).reshape(tuple(lead) + (n * 128,))


def _assemble(results, nl, cores_global, nbatch, ndec):
    f = np.float32
    y_p = np.zeros((nbatch, 2 * T, D), f)
    y_s = np.zeros((ndec, TS, D), f)
    swa_kp = np.zeros((nl, nbatch, 128, 2, 64), f)
    swa_vp = np.zeros((nl, nbatch, 128, 2, 64), f)
    band_kp = np.zeros((nl, nbatch, 512, 8, 64), f)
    band_vp = np.zeros((nl, nbatch, 512, 8, 64), f)
    lru_hp = np.zeros((nl, nbatch, 1024), f)
    lru_cp = np.zeros((nl, nbatch, 3, 1024), f)
    swa_ks = np.zeros((nl, ndec, TS, 2, 64), f)
    swa_vs = np.zeros((nl, ndec, TS, 2, 64), f)
    band_ks = np.zeros((nl, ndec, TS, 8, 64), f)
    band_vs = np.zeros((nl, ndec, TS, 8, 64), f)
    lru_hs = np.zeros((nl, ndec, 1024), f)
    lru_cs = np.zeros((nl, ndec, 3, 1024), f)
    for ci, g in enumerate(cores_global):
        r = results[ci]
        b, half = g // 2, g % 2
        yT = r["yT"].reshape(128, 16, NT).transpose(1, 0, 2).reshape(D, NT)
        y_p[b, half * T:(half + 1) * T] = yT[:, :T].T
        y_s[g] = yT[:, T:].T
        if half == 1:
            kc = r["kc_o"].reshape(nl, 512, 512)
            band_kp[:, b] = kc.transpose(0, 2, 1).reshape(nl, 512, 8, 64)
            band_vp[:, b] = r["vc_o"].reshape(nl, 512, 8, 64)
            ka = r["ka_o"].reshape(nl, 2, 128, 128)[:, :, 0:64, :]
            swa_kp[:, b] = ka.transpose(0, 3, 1, 2)
            va = r["va_o"].reshape(nl, 128, 4, 64)[:, :, [0, 2], :]
            swa_vp[:, b] = va
            lru_hp[:, b] = _unpp(r["hl_o"].reshape(128, nl, 8), (nl,), 8)
            cl = r["cl_o"].reshape(128, nl, 8, 3).transpose(0, 1, 3, 2)
            lru_cp[:, b] = _unpp(cl, (nl, 3), 8)
        kcs = r["kcs_o"].reshape(nl, 512, TS)
        band_ks[:, g] = kcs.transpose(0, 2, 1).reshape(nl, TS, 8, 64)
        band_vs[:, g] = r["vcs_o"].reshape(nl, TS, 8, 64)
        kas = r["kas_o"].reshape(nl, 2, 128, TS)[:, :, 0:64, :]
        swa_ks[:, g] = kas.transpose(0, 3, 1, 2)
        swa_vs[:, g] = r["vas_o"].reshape(nl, TS, 4, 64)[:, :, [0, 2], :]
        lru_hs[:, g] = _unpp(r["hls_o"].reshape(128, nl, 8), (nl,), 8)
        cls = r["cls_o"].reshape(128, nl, 8, 3).transpose(0, 1, 3, 2)
        lru_cs[:, g] = _unpp(cls, (nl, 3), 8)
    return (y_p, y_s, swa_kp, swa_vp, band_kp, band_vp, lru_hp, lru_cp,
            swa_ks, swa_vs, band_ks, band_vs, lru_hs, lru_cs)


def kernel(**inputs):
    results = _run(inputs, NLAYER, list(range(8)))
    return _assemble(results, NLAYER, list(range(8)), 4, 8)
```
